# Optimizing a Trainium2 kernel written in Bass

```python
import jax, jax.numpy as jnp
from jax import lax
import numpy as np

D_MODEL = 1024
BATCH = 8
SEQ = 2048
DEPTH = 4
DEC_BATCH = 128
DEC_SEQ = 1
PAST_LEN = 16384
PAGE_SIZE = 128

D_A = D_MODEL
CHUNK = 128
GW_A = 128
G_A = D_A // GW_A
D_B = D_MODEL
H_B = 16
BW_B = D_B // H_B
CONV_W = 4
C_RG = 8.0
EPS = 1e-6
SPLIT_SIZES = (D_A, D_A, D_A, D_B, D_B, D_MODEL, D_MODEL)
SPLIT_IDX = tuple(int(s) for s in np.cumsum(SPLIT_SIZES)[:-1])
D_IN = int(sum(SPLIT_SIZES))

kernel_name = "hybrid_gmlp_rglru_decoder_step"


def rmsnorm(x, g):
    xf = x.astype(jnp.float32)
    r = xf * lax.rsqrt(jnp.mean(xf * xf, axis=-1, keepdims=True) + EPS)
    return (r * g.astype(jnp.float32)).astype(x.dtype)


def chunk_spatial_gate(u, v, w_s, b_s):
    B, T, _ = v.shape
    n_chunks = -(-T // CHUNK)
    pad = n_chunks * CHUNK - T
    vp = jnp.pad(v, ((0, 0), (0, pad), (0, 0))).reshape(B, n_chunks, CHUNK, G_A, GW_A)
    mask = jnp.tril(jnp.ones((CHUNK, CHUNK), dtype=bool))
    ws = jnp.where(mask[None], w_s, jnp.zeros_like(w_s))
    s = jnp.einsum('gts,bnsgc->bntgc', ws, vp) + b_s.T[None, None, :, :, None]
    s = s.reshape(B, n_chunks * CHUNK, D_A)[:, :T]
    return u * s


def causal_conv(x, buf, conv_w, conv_b):
    T = x.shape[1]
    xp = jnp.concatenate([buf.astype(x.dtype), x], axis=1)
    out = conv_b
    for k in range(CONV_W):
        out = out + conv_w[k] * xp[:, k:k + T]
    return out, xp[:, -(CONV_W - 1):]


def rg_lru(x, h0, w_a, b_a, w_x, b_x, lam):
    B, T, _ = x.shape
    xh = x.reshape(B, T, H_B, BW_B)
    r = jax.nn.sigmoid(jnp.einsum('bthi,hij->bthj', xh, w_a).reshape(B, T, D_B) + b_a)
    i = jax.nn.sigmoid(jnp.einsum('bthi,hij->bthj', xh, w_x).reshape(B, T, D_B) + b_x)
    log_a = -C_RG * r.astype(jnp.float32) * jax.nn.softplus(-lam.astype(jnp.float32))
    a = jnp.exp(log_a)
    xs = jnp.sqrt(-jnp.expm1(2.0 * log_a)) * (i * x).astype(jnp.float32)

    def step(h, inp):
        a_t, x_t = inp
        h = a_t * h + x_t
        return h, h

    h_last, hs = lax.scan(step, h0.astype(jnp.float32),
                          (jnp.swapaxes(a, 0, 1), jnp.swapaxes(xs, 0, 1)))
    return jnp.swapaxes(hs, 0, 1).astype(x.dtype), h_last


def mixer_layer(x, c, h0, conv_buf, w_ada, b_ada, norm_g, w_in, v_norm_g, w_s, b_s,
                conv_w, conv_b, w_rg_a, b_rg_a, w_rg_x, b_rg_x, lam, w_pa, w_pb, w_out):
    mod = jax.nn.silu(c) @ w_ada + b_ada
    shift, scale, gate = jnp.split(mod, 3, axis=-1)
    h = rmsnorm(x, norm_g) * (1.0 + scale[:, None]) + shift[:, None]
    z = h @ w_in
    u, v, g_a, x_b, g_b, z_a, z_b = jnp.split(z, SPLIT_IDX, axis=-1)
    v = rmsnorm(v, v_norm_g)
    y_a = chunk_spatial_gate(u, v, w_s, b_s) * jax.nn.silu(g_a)
    xc, conv_new = causal_conv(x_b, conv_buf, conv_w, conv_b)
    y_rnn, h_last = rg_lru(xc, h0, w_rg_a, b_rg_a, w_rg_x, b_rg_x, lam)
    y_b = y_rnn * jax.nn.silu(g_b)
    merged = jax.nn.sigmoid(z_a) * (y_a @ w_pa) + jax.nn.sigmoid(z_b) * (y_b @ w_pb)
    out = x + gate[:, None] * (merged @ w_out)
    return out, h_last, conv_new, v


def setup_inputs(seed: int = 0) -> dict:
    key = jax.random.key(seed)
    ks = jax.random.split(key, 24)
    f32 = jnp.float32
    n = lambda k, shape, s: jax.random.normal(k, shape, f32) * s
    a0 = jax.random.uniform(ks[19], (DEPTH, D_B), f32, 0.9, 0.999)
    p = a0 ** (1.0 / C_RG)
    lam = jnp.log(p) - jnp.log1p(-p)
    return {
        "x_prompt": n(ks[0], (BATCH, SEQ, D_MODEL), 1.0),
        "x_sample": n(ks[1], (DEC_BATCH, DEC_SEQ, D_MODEL), 1.0),
        "c_prompt": n(ks[2], (BATCH, D_MODEL), 1.0),
        "c_sample": n(ks[3], (DEC_BATCH, D_MODEL), 1.0),
        "state_rglru_h": n(ks[4], (DEPTH, DEC_BATCH, D_B), 0.5),
        "state_conv": n(ks[5], (DEPTH, DEC_BATCH, CONV_W - 1, D_B), 1.0),
        "w_ada": n(ks[6], (DEPTH, D_MODEL, 3 * D_MODEL), 0.5 * D_MODEL ** -0.5),
        "b_ada": n(ks[7], (DEPTH, 3 * D_MODEL), 0.01),
        "norm_g": 1.0 + n(ks[8], (DEPTH, D_MODEL), 0.05),
        "w_in": n(ks[9], (DEPTH, D_MODEL, D_IN), D_MODEL ** -0.5),
        "v_norm_g": 1.0 + n(ks[10], (DEPTH, D_A), 0.05),
        "w_s": n(ks[11], (DEPTH, G_A, CHUNK, CHUNK), CHUNK ** -0.5),
        "b_s": 1.0 + n(ks[12], (DEPTH, G_A, CHUNK), 0.1),
        "conv_w": n(ks[13], (DEPTH, CONV_W, D_B), CONV_W ** -0.5),
        "conv_b": n(ks[14], (DEPTH, D_B), 0.01),
        "w_rg_a": n(ks[15], (DEPTH, H_B, BW_B, BW_B), BW_B ** -0.5),
        "b_rg_a": n(ks[16], (DEPTH, D_B), 0.01),
        "w_rg_x": n(ks[17], (DEPTH, H_B, BW_B, BW_B), BW_B ** -0.5),
        "b_rg_x": n(ks[18], (DEPTH, D_B), 0.01),
        "lam": lam,
        "w_pa": n(ks[20], (DEPTH, D_A, D_MODEL), D_A ** -0.5),
        "w_pb": n(ks[21], (DEPTH, D_B, D_MODEL), D_B ** -0.5),
        "w_out": n(ks[22], (DEPTH, D_MODEL, D_MODEL), D_MODEL ** -0.5),
        "final_g": 1.0 + n(ks[23], (D_MODEL,), 0.05),
    }


def reference(x_prompt, x_sample, c_prompt, c_sample, state_rglru_h, state_conv,
              w_ada, b_ada, norm_g, w_in, v_norm_g, w_s, b_s, conv_w, conv_b,
              w_rg_a, b_rg_a, w_rg_x, b_rg_x, lam, w_pa, w_pb, w_out, final_g):
    xp, xs = x_prompt, x_sample
    hp_list, cp_list, hs_list, cs_list, vs_list = [], [], [], [], []
    for l in range(DEPTH):
        params = (w_ada[l], b_ada[l], norm_g[l], w_in[l], v_norm_g[l], w_s[l], b_s[l],
                  conv_w[l], conv_b[l], w_rg_a[l], b_rg_a[l], w_rg_x[l], b_rg_x[l],
                  lam[l], w_pa[l], w_pb[l], w_out[l])
        h0_p = jnp.zeros((xp.shape[0], D_B), jnp.float32)
        buf_p = jnp.zeros((xp.shape[0], CONV_W - 1, D_B), xp.dtype)
        xp, hp, cp, _ = mixer_layer(xp, c_prompt, h0_p, buf_p, *params)
        xs, hs, cs, vs = mixer_layer(xs, c_sample, state_rglru_h[l], state_conv[l], *params)
        hp_list.append(hp)
        cp_list.append(cp)
        hs_list.append(hs)
        cs_list.append(cs)
        vs_list.append(vs)
    y_prompt = rmsnorm(xp, final_g)
    y_sample = rmsnorm(xs, final_g)
    h_prompt = jnp.stack(hp_list)
    conv_prompt = jnp.stack(cp_list)
    h_sample = jnp.stack(hs_list)
    conv_sample = jnp.stack(cs_list)
    chunk_v_sample = jnp.stack(vs_list)
    return (y_prompt, y_sample, h_prompt, conv_prompt, h_sample, conv_sample, chunk_v_sample)
```

```python
import numpy as np
from contextlib import ExitStack
import concourse.bass as bass
import concourse.mybir as mybir
from concourse.bass_utils import run_bass_kernel_spmd

F32 = mybir.dt.float32
BF16 = mybir.dt.bfloat16
ALU = mybir.AluOpType
AF = mybir.ActivationFunctionType

NCORES = 8
D = 1024
SEQ = 2048
NS = 16
DEPTH = 4
TT = SEQ + NS
THM = 1040
EPS = 1e-6
NPR = 12
PW = 256
NSLOT = 8
NT2K = 9
NTB = 3


class Buf:
    __slots__ = ("name", "w", "r", "region", "track", "psum")

    def __init__(self, name, region=None, track=True, psum=False):
        self.psum = psum
        self.name = name
        self.w = None
        self.r = []
        self.region = region
        self.track = track


class Region:
    def __init__(self):
        self.fence = []
        self.bufs = []

    def new(self, name):
        b = Buf(name, region=self)
        self.bufs.append(b)
        return b

    def handoff(self):
        deps = list(self.fence)
        for b in self.bufs:
            if b.w is not None:
                deps.append(b.w)
            deps.extend(b.r)
            b.w = None
            b.r = []
        best = {}
        for s, v in deps:
            k = id(s)
            if k not in best or best[k][1] < v:
                best[k] = (s, v)
        self.fence = list(best.values())


class DSem:
    def __init__(self, sem):
        self.sem = sem
        self.n = 0


ENGS = ("pe", "act", "dve", "pool", "sp")


class Builder:
    def __init__(self):
        self.nc = bass.Bass("TRN2", target_bir_lowering=False)
        self.es = ExitStack()
        self.ops = {e: [] for e in ENGS}
        self.cnt = {e: 0 for e in ENGS}
        self.seen = {e: {} for e in ENGS}
        self.esem = {}
        for e in ("pe", "act", "dve", "pool"):
            self.esem[e] = self.es.enter_context(self.nc.semaphore("sem_" + e))
        self.out_dsems = []
        self.nsem = 4

    def sb(self, name, shape, dt):
        return self.es.enter_context(self.nc.sbuf_tensor(name, shape, dt))

    def dsem(self, name):
        self.nsem += 1
        return DSem(self.es.enter_context(self.nc.semaphore(name)))

    def _waits(self, eng, reads, writes):
        deps = {}

        def add(d, raw):
            s, v = d
            k = id(s)
            if k not in deps or deps[k][1] < v:
                deps[k] = (s, v)

        for b in reads:
            if b.w is not None:
                add(b.w, True)
            if b.psum:
                for d in b.r:
                    add(d, False)
        for b in writes:
            if b.w is not None:
                add(b.w, False)
            for d in b.r:
                add(d, False)
            if b.region is not None:
                for d in b.region.fence:
                    add(d, False)
        waits = []
        seen = self.seen[eng]
        for k, (s, v) in deps.items():
            if seen.get(k, 0) >= v:
                continue
            seen[k] = v
            waits.append((s, v))
        return waits

    def emit(self, eng, fns, reads=(), writes=()):
        if not isinstance(fns, (list, tuple)):
            fns = [fns]
        waits = self._waits(eng, reads, writes)
        self.cnt[eng] += 1
        dep = (self.esem[eng], self.cnt[eng])
        for b in reads:
            if b.track:
                b.r.append(dep)
        for b in writes:
            b.w = dep
            b.r = []
        self.ops[eng].append((waits, list(fns), (self.esem[eng], 1)))

    def dma(self, queue, fn, ds, reads=(), writes=()):
        waits = self._waits(queue, reads, writes)
        ds.n += 1
        dep = (ds.sem, 16 * ds.n)
        for b in reads:
            if b.track:
                b.r.append(dep)
        for b in writes:
            b.w = dep
            b.r = []
        self.ops[queue].append((waits, [fn], (ds.sem, 16)))

    def mark(self, name):
        if not hasattr(self, "marks"):
            self.marks = []
        self.marks.append((name, {e: sum(len(o[1]) for o in self.ops[e]) for e in ENGS}))

    def replay(self, eng, h):
        for waits, fns, inc in self.ops[eng]:
            for s, v in waits:
                h.wait_ge(s, v)
            n = len(fns)
            for i, f in enumerate(fns):
                ins = f(h)
                if i == n - 1 and inc is not None:
                    ins.then_inc(inc[0], inc[1])


import os as _osx


def build_program():
    B = Builder()
    nc = B.nc
    sb = B.sb

    def din(name, shape):
        return nc.dram_tensor(name, shape, F32, kind="ExternalInput").ap()

    def dout(name, shape):
        return nc.dram_tensor(name, shape, F32, kind="ExternalOutput").ap()

    xp_d = din("xp", [SEQ, D])
    xs_d = din("xs", [NS, D])
    c17_d = din("c17", [17, D])
    sh_d = din("sh", [DEPTH, NS, D])
    sc_d = din("sc", [DEPTH, NS, 3, D])
    w_ada_d = din("w_ada", [DEPTH, D, 3 * D])
    b_ada_d = din("b_ada", [DEPTH, 3 * D])
    norm_g_d = din("norm_g", [DEPTH, D])
    w_in_d = din("w_in", [DEPTH, D, 7 * D])
    v_norm_g_d = din("v_norm_g", [DEPTH, D])
    w_s_d = din("w_s", [DEPTH, 8, 128, 128])
    b_s_d = din("b_s", [DEPTH, 8, 128])
    conv_w_d = din("conv_w", [DEPTH, 4, D])
    conv_b_d = din("conv_b", [DEPTH, D])
    w_rg_a_d = din("w_rg_a", [DEPTH, 16, 64, 64])
    b_rg_a_d = din("b_rg_a", [DEPTH, D])
    w_rg_x_d = din("w_rg_x", [DEPTH, 16, 64, 64])
    b_rg_x_d = din("b_rg_x", [DEPTH, D])
    lam_d = din("lam", [DEPTH, D])
    w_pa_d = din("w_pa", [DEPTH, D, D])
    w_pb_d = din("w_pb", [DEPTH, D, D])
    w_out_d = din("w_out", [DEPTH, D, D])
    final_g_d = din("final_g", [1, D])
    ident_d = din("ident", [128, 128])
    mask_d = din("mask", [128, 128])

    yp_d = dout("yp", [SEQ, D])
    ys_d = dout("ys", [NS, D])
    hp_d = dout("hp", [DEPTH, D])
    cp_d = dout("cp", [DEPTH, 3, D])
    hs_d = dout("hs", [DEPTH, NS, D])
    cs_d = dout("cs", [DEPTH, NS, 3, D])
    cv_d = dout("cv", [DEPTH, NS, D])

    X = sb("X", [128, 8, THM], F32)
    Xb = [[Buf(f"X{j}_{t}") for t in range(3)] for j in range(8)]
    H = sb("H", [128, 8, THM], BF16)
    Hb = [Buf(f"H{t}") for t in range(3)]
    regVM = Region()
    VM = sb("VM", [128, 9 * 1024], BF16)
    Vv = VM[:, :].rearrange("p (c f) -> p c f", c=9)
    Vb = [regVM.new(f"V{c}") for c in range(9)]
    Mv = VM[:, 0:8 * THM].rearrange("p (j t) -> p j t", j=8)
    Mb = [regVM.new(f"M{t}") for t in range(3)]
    ROWS = sb("ROWS", [128, 5, THM], F32)
    ROW_A = [ROWS[:, 0, :], ROWS[:, 1, :]]
    ROW_S = ROWS[:, 2, :]
    ROW_P = [ROWS[:, 3, :], ROWS[:, 4, :]]
    rowAb = [[Buf(f"rowA{r}_{t}") for t in range(3)] for r in range(2)]
    rowSb = [Buf(f"rowS{t}") for t in range(3)]
    rowPb = [[Buf(f"rowP{r}_{t}") for t in range(3)] for r in range(2)]
    XBv = sb("XB", [128, 3 + THM + 1], BF16)
    XBb = Buf("XB")
    YA = sb("YA", [128, 8, THM], BF16)
    YAb = [[Buf(f"YA{j}_{t}") for t in range(3)] for j in range(8)]
    YB = sb("YB", [128, 8, THM], BF16)
    YBb = [[Buf(f"YB{j}_{t}") for t in range(3)] for j in range(8)]

    slots = [sb(f"slot{i}", [128, 8, PW], BF16) for i in range(NSLOT)]
    slotb = [Buf(f"slot{i}") for i in range(NSLOT)]
    slotsem = [B.dsem(f"slotsem{i}") for i in range(NSLOT)]
    slot_ctr = [0]

    t2k = [sb(f"t2k{i}", [128, 512], F32) for i in range(NT2K)]
    t2kb = [Buf(f"t2k{i}") for i in range(NT2K)]
    t2k_ctr = [0]
    tb = [sb(f"tb{i}", [128, 512], BF16) for i in range(NTB)]
    tbb = [Buf(f"tb{i}") for i in range(NTB)]
    tb_ctr = [0]

    ps = [B.es.enter_context(nc.psum_tensor(f"ps{i}", [128, 512], F32)) for i in range(8)]
    psb = [Buf(f"ps{i}", psum=True) for i in range(8)]
    ps_ctr = [0]

    def T2K():
        i = t2k_ctr[0] % NT2K
        t2k_ctr[0] += 1
        return t2k[i], t2kb[i]

    def TB():
        i = tb_ctr[0] % NTB
        tb_ctr[0] += 1
        return tb[i], tbb[i]

    ps_pinned = [False] * 7

    def PS(pin=False):
        for _ in range(7):
            i = ps_ctr[0] % 7
            ps_ctr[0] += 1
            if not ps_pinned[i]:
                break
        else:
            raise AssertionError("all PSUM banks pinned")
        if pin:
            ps_pinned[i] = True
        return ps[i], psb[i]

    def PS_release(*bufs):
        for b_ in bufs:
            ps_pinned[psb.index(b_)] = False

    GB = sb("GB", [128, D], F32)
    GBb = Buf("GB")
    PRM = sb("PRM", [128, 8, NPR * DEPTH], F32)
    PRMb = Buf("PRM", track=False)
    IDENT = sb("IDENT", [128, 128], F32)
    IDENTB = sb("IDENTB", [128, 128], BF16)
    MASK = sb("MASK", [128, 128], F32)
    ONESB = sb("ONESB", [128, 128], BF16)
    CONb = Buf("consts", track=False)
    NSP = sb("NSP", [128, 2, DEPTH, 8], F32)
    HB = sb("HB", [128, 2, DEPTH, 8], F32)
    NSPb = Buf("NSP", track=False)
    SMALL = sb("SMALL", [128, 64], F32)
    SMALLb = Buf("SMALL")
    CT = sb("CT", [128, 8, 17], BF16)
    CTb = Buf("CT", track=False)
    MOD = [sb(f"MOD{i}", [128, 3, 8, 17], F32) for i in range(DEPTH)]
    MODb = [Buf(f"MOD{i}", track=False) for i in range(DEPTH)]
    DER = [sb(f"DER{i}", [128, 2, 8], F32) for i in range(DEPTH)]
    DERS1 = sb("DERS", [128, 2, 8, NS], F32)
    DERS = [DERS1 for i in range(DEPTH)]
    DERSb = Buf("DERS")
    DERb = [Buf(f"DER{i}", track=False) for i in range(DEPTH)]
    RG = sb("RG", [128, 8, 2, 128], BF16)
    RGb = Buf("RG")
    DG = sb("DG", [128, 2, 4, 128], BF16)
    DGb = [Buf("DG0"), Buf("DG1")]
    dg_ctr = [0]
    WST = sb("WST", [128, 8, 128], BF16)
    WSTb = Buf("WST")
    WSTS = sb("WSTS", [16, 8, 16], BF16)
    W00 = sb("W00", [16, 8], F32)
    WSTSb = Buf("WSTS")
    BS2 = sb("BS2", [2, 8, 128], BF16)
    BS2S = sb("BS2S", [2, 8, NS], BF16)
    BSb = Buf("BShi")
    BSLb = Buf("BSlo")
    SCVA = sb("SCVA", [128, DEPTH, 8, 3, NS], BF16)
    SCN = sb("SCN", [128, 8, NS], BF16)
    SCVb = Buf("SCV", track=False)
    SCNb = Buf("SCN")
    H0SA = sb("H0SA", [128, DEPTH, 8, NS], F32)
    H0Sb = Buf("H0S", track=False)
    HSS = sb("HSS", [128, 8, NS], F32)
    HSSb = Buf("HSS")
    OTMP = sb("OTMP", [128, NS], F32)
    OTMPb = Buf("OTMP")
    CSS = sb("CSS", [128, 8, NS], F32)
    CSSb = Buf("CSS")
    HP = sb("HP", [128, 8], F32)
    HPb = Buf("HP")
    CPT = sb("CPT", [128, 8, 3], F32)
    CPTb = Buf("CPT")
    HCL = sb("HCL", [128, DEPTH, 8], F32)
    HCb = Buf("HC")
    CXBL = sb("CXBL", [128, DEPTH, 8, 3], BF16)
    CXBb = Buf("CXB")

    init_ds = B.dsem("init")
    in_ds = [B.dsem(f"in{i}") for i in range(NT2K)]
    misc_ds = {}

    def mds(name):
        if name not in misc_ds:
            misc_ds[name] = B.dsem("ds_" + name)
        return misc_ds[name]

    out_ds = []

    def ods(name):
        d = B.dsem("o_" + name)
        out_ds.append(d)
        return d

    buf_ods = {}

    def ods_for(buf):
        if id(buf) not in buf_ods:
            buf_ods[id(buf)] = ods("b_" + buf.name)
        return buf_ods[id(buf)]

    def T2K_in():
        i = t2k_ctr[0] % NT2K
        t2k_ctr[0] += 1
        return t2k[i], t2kb[i], in_ds[i]

    wview = {
        "w_in": [w_in_d[l].rearrange("(c p) n -> p c n", p=128) for l in range(DEPTH)],
        "w_pa": [w_pa_d[l].rearrange("(c p) n -> p c n", p=128) for l in range(DEPTH)],
        "w_pb": [w_pb_d[l].rearrange("(c p) n -> p c n", p=128) for l in range(DEPTH)],
        "w_out": [w_out_d[l].rearrange("(c p) n -> p c n", p=128) for l in range(DEPTH)],
        "w_ada": [w_ada_d[l].rearrange("(c p) n -> p c n", p=128) for l in range(DEPTH)],
    }

    slot_held = [False] * NSLOT

    def load_piece(which, l, c0):
        for _ in range(NSLOT):
            i = slot_ctr[0] % NSLOT
            slot_ctr[0] += 1
            if not slot_held[i]:
                break
        else:
            raise AssertionError("weight ring exhausted")
        slot_held[i] = True
        src = wview[which][l][:, :, c0:c0 + PW]
        dst = slots[i]
        B.dma("pool", lambda h, dst=dst, src=src: h.dma_start(out=dst[:], in_=src), slotsem[i],
              writes=[slotb[i]])
        return slots[i], slotb[i]

    def release_piece(*bufs):
        for b_ in bufs:
            slot_held[slotb.index(b_)] = False

    def mm(out, lhsT, rhs, start, stop):
        return lambda h: h.matmul(out, lhsT=lhsT, rhs=rhs, start=start, stop=stop)

    def tr(out, in_, ident):
        return lambda h: h.transpose(out=out, in_=in_, identity=ident)

    def act(out, in_, func, bias=None, scale=None, accum_out=None):
        kw = {}
        if bias is not None:
            kw["bias"] = bias
        if scale is not None:
            kw["scale"] = scale
        if accum_out is not None:
            kw["accum_out"] = accum_out
        return lambda h: h.activation(out=out, in_=in_, func=func, **kw)

    def stt(out, in0, scalar, in1, op0, op1):
        return lambda h: h.scalar_tensor_tensor(out=out, in0=in0, scalar=scalar, in1=in1, op0=op0, op1=op1)

    def tt(out, in0, in1, op):
        return lambda h: h.tensor_tensor(out=out, in0=in0, in1=in1, op=op)

    def ts(out, in0, s1, s2, op0, op1=None):
        if op1 is None:
            return lambda h: h.tensor_scalar(out=out, in0=in0, scalar1=s1, scalar2=None, op0=op0)
        return lambda h: h.tensor_scalar(out=out, in0=in0, scalar1=s1, scalar2=s2, op0=op0, op1=op1)

    def cp(out, in_):
        return lambda h: h.tensor_copy(out=out, in_=in_)

    B.dma("sp", lambda h: h.dma_start(out=IDENT[:], in_=ident_d), init_ds, writes=[CONb])
    B.dma("sp", lambda h: h.dma_start(out=MASK[:], in_=mask_d), init_ds, writes=[CONb])
    n_init = [2]

    def wait_init(eng):
        CONb.w = (init_ds.sem, 16 * init_ds.n)

    B.emit("dve", cp(IDENTB[:], IDENT[:]), reads=[CONb], writes=[Buf("tmp")])
    B.emit("dve", lambda h: h.memset(ONESB[:], 1.0), writes=[Buf("tmp")])
    B.emit("dve", lambda h: h.memset(RG[:].rearrange("p a b c -> p (a b c)"), 0.0), writes=[RGb])
    conb2 = Buf("consts2", track=False)
    B.emit("dve", lambda h: h.memset(SMALL[:], 0.0), writes=[SMALLb, conb2])

    prm_src = [
        lambda hh: norm_g_d[:, hh * 512:(hh + 1) * 512],
        lambda hh: conv_w_d[:, 0, hh * 512:(hh + 1) * 512],
        lambda hh: conv_w_d[:, 1, hh * 512:(hh + 1) * 512],
        lambda hh: conv_w_d[:, 2, hh * 512:(hh + 1) * 512],
        lambda hh: conv_w_d[:, 3, hh * 512:(hh + 1) * 512],
        lambda hh: conv_b_d[:, hh * 512:(hh + 1) * 512],
        lambda hh: b_rg_a_d[:, hh * 512:(hh + 1) * 512],
        lambda hh: b_rg_x_d[:, hh * 512:(hh + 1) * 512],
        lambda hh: lam_d[:, hh * 512:(hh + 1) * 512],
        lambda hh: b_ada_d[:, 0 * D + hh * 512:0 * D + (hh + 1) * 512],
        lambda hh: b_ada_d[:, 1 * D + hh * 512:1 * D + (hh + 1) * 512],
        lambda hh: b_ada_d[:, 2 * D + hh * 512:2 * D + (hh + 1) * 512],
    ]
    R_NG, R_CW, R_CB, R_BA, R_BX, R_LAM, R_ADA = 0, 1, 5, 6, 7, 8, 9
    NR = NPR * DEPTH

    def prow(k, l):
        return k * DEPTH + l

    pp, ppb = PS()
    stg = []
    for hh in range(2):
        t, tbuf, tds = T2K_in()
        for k in range(NPR):
            src = prm_src[k](hh)
            B.dma("sp", lambda h, t=t, k=k, src=src: h.dma_start(out=t[k * DEPTH:(k + 1) * DEPTH, :], in_=src),
                  tds, writes=[tbuf])
        stg.append((t, tbuf))
    for hh in range(2):
        t, tbuf = stg[hh]
        fns = [tr(pp[:, (hh * 4 + q) * NR:(hh * 4 + q + 1) * NR], t[0:NR, q * 128:(q + 1) * 128], IDENT[0:NR, 0:NR])
               for q in range(4)]
        B.emit("pe", fns, reads=[tbuf, conb2], writes=[ppb])
    B.emit("act", act(PRM[:].rearrange("p j r -> p (j r)"), pp[:, 0:8 * NR], AF.Copy), reads=[ppb], writes=[PRMb])

    def P(j, k, l):
        r = prow(k, l)
        return PRM[:, j, r:r + 1]

    lamv = PRM[:, :, prow(R_LAM, 0):prow(R_LAM, 0) + DEPTH]
    E_ = SMALL[:, 0:32].rearrange("p (j l) -> p j l", j=8)
    U_ = SMALL[:, 32:64].rearrange("p (j l) -> p j l", j=8)
    t1, t1b = T2K()
    D_ = t1[:, 0:32].rearrange("p (j l) -> p j l", j=8)
    Q_ = t1[:, 32:64].rearrange("p (j l) -> p j l", j=8)
    L_ = t1[:, 64:96].rearrange("p (j l) -> p j l", j=8)
    N_ = t1[:, 96:128].rearrange("p (j l) -> p j l", j=8)
    R_ = t1[:, 128:160].rearrange("p (j l) -> p j l", j=8)
    S_ = t1[:, 160:192].rearrange("p (j l) -> p j l", j=8)
    B.emit("act", act(E_, lamv, AF.Exp, scale=-1.0), reads=[PRMb], writes=[SMALLb])
    B.emit("dve", ts(U_, E_, 1.0, None, ALU.add), reads=[SMALLb], writes=[SMALLb])
    B.emit("dve", ts(D_, U_, -1.0, None, ALU.add), reads=[SMALLb], writes=[t1b])
    B.emit("dve", ts(Q_, D_, 0.0, None, ALU.is_equal), reads=[t1b], writes=[t1b])
    B.emit("act", act(L_, U_, AF.Ln), reads=[SMALLb, t1b], writes=[t1b])
    B.emit("dve", tt(N_, L_, Q_, ALU.add), reads=[t1b], writes=[t1b])
    B.emit("dve", tt(R_, D_, Q_, ALU.add), reads=[t1b], writes=[t1b])
    B.emit("dve", lambda h: h.reciprocal(out=R_, in_=R_), reads=[t1b], writes=[t1b])
    B.emit("dve", tt(S_, N_, R_, ALU.mult), reads=[t1b], writes=[t1b])
    B.emit("dve", tt(S_, S_, E_, ALU.mult), reads=[t1b, SMALLb], writes=[t1b])
    B.emit("dve", ts(NSP[:, 0].rearrange("p l j -> p j l"), S_, -4.0, None, ALU.mult), reads=[t1b], writes=[NSPb])
    B.emit("dve", ts(NSP[:, 1].rearrange("p l j -> p j l"), S_, -8.0, None, ALU.mult), reads=[t1b], writes=[NSPb])
    B.emit("dve", ts(HB[:, 0].rearrange("p l j -> p j l"), PRM[:, :, prow(R_BA, 0):prow(R_BA, 0) + DEPTH], 0.5, None,
                     ALU.mult), reads=[PRMb], writes=[NSPb])
    B.emit("dve", ts(HB[:, 1].rearrange("p l j -> p j l"), PRM[:, :, prow(R_BX, 0):prow(R_BX, 0) + DEPTH], 0.5, None,
                     ALU.mult), reads=[PRMb], writes=[NSPb])

    pc, pcb = PS()
    for hh in range(2):
        t, tbuf, tds = T2K_in()
        B.dma("sp", lambda h, t=t, hh=hh: h.dma_start(out=t[0:17, :], in_=c17_d[:, hh * 512:(hh + 1) * 512]), tds,
              writes=[tbuf])
        t2, t2b = T2K()
        B.emit("act", act(t2[0:17, :], t[0:17, :], AF.Tanh, scale=0.5), reads=[tbuf], writes=[t2b])
        B.emit("dve", stt(t2[0:17, :], t2[0:17, :], 1.0, t[0:17, :], ALU.add, ALU.mult), reads=[t2b, tbuf],
               writes=[t2b])
        fns = [tr(pc[:, (hh * 4 + q) * 17:(hh * 4 + q + 1) * 17], t2[0:17, q * 128:(q + 1) * 128], IDENT[0:17, 0:17])
               for q in range(4)]
        B.emit("pe", fns, reads=[t2b, conb2], writes=[pcb])
    B.emit("act", act(CT[:].rearrange("p j r -> p (j r)"), pc[:, 0:8 * 17], AF.Copy, scale=0.5), reads=[pcb],
           writes=[CTb])

    def emit_mod(l):
        B.mark(f'mod{l}')
        m = l
        pm, pmb = ps[7], psb[7]
        for pi in range(3 * D // PW):
            sl, slb = load_piece("w_ada", l, pi * PW)
            for s in range(PW // 128):
                fc = pi * (PW // 128) + s
                fns = [mm(pm[:, fc * 17:(fc + 1) * 17], sl[:, kc, s * 128:(s + 1) * 128], CT[:, kc, :],
                          (kc == 0), (kc == 7)) for kc in range(8)]
                B.emit("pe", fns, reads=[slb, CTb], writes=[pmb])
            release_piece(slb)
            yield
        for sec in range(3):
            bias = PRM[:, :, prow(R_ADA + sec, l):prow(R_ADA + sec, l) + 1].to_broadcast([128, 8, 17])
            B.emit("dve", tt(MOD[m][:, sec], pm[:, sec * 136:(sec + 1) * 136].rearrange("p (j r) -> p j r", j=8),
                             bias, ALU.add), reads=[pmb, PRMb], writes=[MODb[m]])
        ng = PRM[:, :, prow(R_NG, l)]
        B.emit("dve", stt(DER[m][:, 0], MOD[m][:, 1, :, 0], 1.0, ng, ALU.add, ALU.mult), reads=[MODb[m], PRMb],
               writes=[DERb[m]])
        B.emit("dve", ts(DER[m][:, 1], MOD[m][:, 2, :, 0], 0.5, None, ALU.mult), reads=[MODb[m]], writes=[DERb[m]])

    def emit_ders(l):
        m = l
        ngb = PRM[:, :, prow(R_NG, l):prow(R_NG, l) + 1].to_broadcast([128, 8, NS])
        B.emit("dve", stt(DERS1[:, 0], MOD[m][:, 1, :, 1:17], 1.0, ngb, ALU.add, ALU.mult), reads=[MODb[m], PRMb],
               writes=[DERSb])
        B.emit("dve", ts(DERS1[:, 1], MOD[m][:, 2, :, 1:17], 0.5, None, ALU.mult), reads=[MODb[m]],
               writes=[DERSb])

    def load_x(hf):
        B.mark(f'loadx{hf}')
        nch = 8 + (1 if hf == 1 else 0)
        for c in range(nch):
            n = 128 if c < 8 else NS
            col0 = c * 128
            for hh in range(2):
                t, tbuf, tds = T2K_in()
                if c < 8:
                    r0 = (hf * 8 + c) * 128
                    src = xp_d[r0:r0 + 128, hh * 512:(hh + 1) * 512]
                else:
                    src = xs_d[:, hh * 512:(hh + 1) * 512]
                B.dma("sp", lambda h, t=t, n=n, src=src: h.dma_start(out=t[0:n, :], in_=src), tds, writes=[tbuf])
                p_, pb_ = PS()
                fns = [tr(p_[:, q * n:(q + 1) * n], t[0:n, q * 128:(q + 1) * 128], IDENT[0:n, 0:n]) for q in range(4)]
                B.emit("pe", fns, reads=[tbuf, conb2], writes=[pb_])
                tile = min(col0 // 512, 2)
                B.emit("act", act(X[:, hh * 4:(hh + 1) * 4, col0:col0 + n],
                                  p_[:, 0:4 * n].rearrange("p (q t) -> p q t", q=4), AF.Copy),
                       reads=[pb_], writes=[Xb[hh * 4 + q][tile] for q in range(4)])

    SETUP_H = {}

    def layer_setup_dma(l, hf):
        B.dma("sp", lambda h: h.dma_start(out=GB[:], in_=v_norm_g_d[l:l + 1, :].to_broadcast([128, D])), mds("gb"),
              writes=[GBb])
        wst = []
        for hh in range(2):
            t, tbuf, tds = T2K_in()
            B.dma("sp", lambda h, t=t, hh=hh: h.dma_start(
                out=t[:, :].rearrange("p (g s) -> p g s", g=4),
                in_=w_s_d[l, hh * 4:(hh + 1) * 4].rearrange("g t s -> t g s")), tds, writes=[tbuf])
            wst.append((t, tbuf))
        B.dma("sp", lambda h: h.dma_start(out=W00[:, :], in_=w_s_d[l, :, 0, 0:1].rearrange("g o -> o g").to_broadcast([16, 8]),
                                          allow_slow_non_contiguous=True),
              mds("w00"), writes=[WSTSb])
        bst = []
        for hh in range(2):
            t, tbuf, tds = T2K_in()
            B.dma("sp", lambda h, t=t, hh=hh: h.dma_start(
                out=t[0:1, :], in_=b_s_d[l:l + 1, hh * 4:(hh + 1) * 4].rearrange("o g t -> o (g t)")), tds,
                writes=[tbuf])
            bst.append((t, tbuf))
        for gi, wd in enumerate((w_rg_a_d, w_rg_x_d)):
            for two in range(2):
                src = wd[l].rearrange("(j two) i o -> two i j o", two=2)[two]
                dst = RG[two * 64:(two + 1) * 64, :, gi, two * 64:(two + 1) * 64]
                B.dma("pool", lambda h, dst=dst, src=src: h.dma_start(out=dst, in_=src), mds("rg"), writes=[RGb])
        SETUP_H[(l, hf)] = (wst, bst)

    def layer_setup_compute(l, hf):
        B.mark(f'setup{l}{hf}')
        wst, bst = SETUP_H.pop((l, hf))
        for hh in range(2):
            t, tbuf = wst[hh]
            p_, pb_ = PS()
            fns = [tr(p_[:, q * 128:(q + 1) * 128], t[:, q * 128:(q + 1) * 128], IDENT[:, :]) for q in range(4)]
            B.emit("pe", fns, reads=[tbuf, conb2], writes=[pb_])
            for q in range(4):
                B.emit("dve", tt(WST[:, hh * 4 + q, :], p_[:, q * 128:(q + 1) * 128], MASK[:, :], ALU.mult),
                       reads=[pb_, conb2], writes=[WSTb])
        for g in range(8):
            B.emit("dve", ts(WSTS[:, g, :], IDENT[0:16, 0:16], W00[:, g:g + 1], None, ALU.mult),
                   reads=[WSTSb, conb2], writes=[WSTSb])
        lo_t, lob = T2K()
        lo_all = lo_t[:, :].bitcast(BF16)
        los_t, losb = T2K()
        los = los_t[:, :].bitcast(BF16)
        for hh in range(2):
            t, tbuf = bst[hh]
            lo = lo_all[:, hh * 512:(hh + 1) * 512]
            bsh = BS2[0:1, hh * 4:(hh + 1) * 4, :].rearrange("o g t -> o (g t)")
            B.emit("dve", cp(bsh, t[0:1, :]), reads=[tbuf], writes=[BSb])
            B.emit("dve", tt(lo[0:1, :], t[0:1, :], bsh, ALU.subtract), reads=[tbuf, BSb], writes=[lob])
            B.emit("dve", cp(los[0:1, hh * 4 * NS:(hh + 1) * 4 * NS].rearrange("o (g t) -> o g t", g=4),
                             lo[0:1, :].rearrange("o (g t) -> o g t", g=4)[:, :, 0:1].to_broadcast([1, 4, NS])),
                   reads=[lob], writes=[losb])
            B.dma("sp", lambda h, hh=hh, lo=lo: h.dma_start(
                out=BS2[1:2, hh * 4:(hh + 1) * 4, :].rearrange("o g t -> o (g t)"), in_=lo[0:1, :]), mds("bslo"),
                reads=[lob], writes=[BSLb])
        B.emit("dve", cp(BS2S[0:1], BS2[0:1, :, 0:1].to_broadcast([1, 8, NS])), reads=[BSb], writes=[BSb])
        B.dma("sp", lambda h, los=los: h.dma_start(out=BS2S[1:2].rearrange("o g t -> o (g t)"), in_=los[0:1, 0:8 * NS]),
              mds("bslo"), reads=[losb], writes=[BSLb])

    def load_sample_state():
        B.mark('sstate')
        for l in range(DEPTH):
            p_, pb_ = PS()
            for hh in range(2):
                t, tbuf, tds = T2K_in()
                B.dma("sp", lambda h, t=t, hh=hh, l=l: h.dma_start(out=t[0:NS, :], in_=sh_d[l, :, hh * 512:(hh + 1) * 512]),
                      tds, writes=[tbuf])
                fns = [tr(p_[:, (hh * 4 + q) * NS:(hh * 4 + q + 1) * NS], t[0:NS, q * 128:(q + 1) * 128],
                          IDENT[0:NS, 0:NS]) for q in range(4)]
                B.emit("pe", fns, reads=[tbuf, conb2], writes=[pb_])
            B.emit("act", act(H0SA[:, l].rearrange("p j b -> p (j b)"), p_[:, 0:8 * NS], AF.Copy), reads=[pb_],
                   writes=[H0Sb])
            p_, pb_ = PS()
            for k in range(3):
                for hh in range(2):
                    t, tbuf, tds = T2K_in()
                    B.dma("sp", lambda h, t=t, hh=hh, k=k, l=l: h.dma_start(
                        out=t[0:NS, :], in_=sc_d[l, :, k, hh * 512:(hh + 1) * 512]), tds, writes=[tbuf])
                    fns = [tr(p_[:, (k * 8 + hh * 4 + q) * NS:(k * 8 + hh * 4 + q + 1) * NS],
                              t[0:NS, q * 128:(q + 1) * 128], IDENT[0:NS, 0:NS]) for q in range(4)]
                    B.emit("pe", fns, reads=[tbuf, conb2], writes=[pb_])
            for k in range(3):
                B.emit("act", act(SCVA[:, l, :, k, :],
                                  p_[:, k * 8 * NS:(k + 1) * 8 * NS].rearrange("p (j b) -> p j b", j=8), AF.Copy),
                       reads=[pb_], writes=[SCVb])
            B.dma("sp", lambda h, l=l: h.dma_start(out=cs_d[l, :, 0:2, :], in_=sc_d[l, :, 1:3, :]), ods(f"cs01_{l}"))

    def tiles_of(hf):
        return [(0, 512), (512, 512)] + ([(1024, NS)] if hf == 1 else [])

    def xtile(hf, ti):
        return ti

    def half_layer(l, hf, mg=None):
        def mod_step():
            if mg is not None:
                next(mg, None)

        m = l
        c0 = 0
        HC = HCL[:, l, :]
        CXB = CXBL[:, l, :, :]
        tiles = tiles_of(hf)
        nt = len(tiles)

        if l > 0:
            layer_setup_compute(l, hf)
        vsl = [load_piece("w_in", l, D + p * PW) for p in range(4)]
        B.mark(f'N{l}{hf}')
        npss = []
        for ti, (off, W) in enumerate(tiles):
            xt = xtile(hf, ti)
            pss, pssb = PS()
            npss.append((pss, pssb))
            for j in range(8):
                sq, sqb = TB()
                xin = X[:, j, c0 + off:c0 + off + W]
                if j % 2 == 0:
                    B.emit("act", act(sq[:, 0:W], xin, AF.Square), reads=[Xb[j][xt]], writes=[sqb])
                else:
                    B.emit("pool", tt(sq[:, 0:W], xin, xin, ALU.mult), reads=[Xb[j][xt]], writes=[sqb])
                B.emit("pe", mm(pss[:, 0:W], ONESB[:, :], sq[:, 0:W], (j == 0), (j == 7)), reads=[sqb, conb2],
                       writes=[pssb])
        if l == 0:
            layer_setup_dma(0, hf)
            layer_setup_compute(0, hf)
        for ti, (off, W) in enumerate(tiles):
            pss, pssb = npss[ti]
            rt, rtb = T2K()
            B.emit("act", act(rt[:, 0:W], pss[:, 0:W], AF.Ln, scale=1.0 / D, bias=SMALL[:, 0:1]), reads=[pssb, epsb],
                   writes=[rtb])
            B.emit("act", act(pss[:, 0:W], rt[:, 0:W], AF.Exp, scale=-0.5), reads=[rtb], writes=[pssb])
        for ti, (off, W) in enumerate(tiles):
            xt = xtile(hf, ti)
            pss, pssb = npss[ti]
            for j in range(8):
                xn, xnb = T2K()
                if W == 512:
                    B.emit("dve", stt(xn[:, 0:W], X[:, j, c0 + off:c0 + off + W], DER[m][:, 0, j:j + 1], pss[:, 0:W],
                                      ALU.mult, ALU.mult), reads=[Xb[j][xt], pssb, DERb[m]], writes=[xnb])
                    B.emit("act", act(H[:, j, off:off + W], xn[:, 0:W], AF.Identity, bias=MOD[m][:, 0, j, 0:1]),
                           reads=[xnb, MODb[m]], writes=[Hb[ti]])
                else:
                    B.emit("dve", tt(xn[:, 0:W], X[:, j, c0 + off:c0 + off + W], pss[:, 0:W], ALU.mult),
                           reads=[Xb[j][xt], pssb], writes=[xnb])
                    B.emit("dve", tt(xn[:, 0:W], xn[:, 0:W], DERS[m][:, 0, j, :], ALU.mult), reads=[xnb, DERSb],
                           writes=[xnb])
                    B.emit("dve", tt(H[:, j, off:off + W], xn[:, 0:W], MOD[m][:, 0, j, 1:17], ALU.add),
                           reads=[xnb, MODb[m]], writes=[Hb[ti]])

        B.mark(f'V{l}{hf}')
        regVM.handoff()
        nchunks = 8 + (1 if nt == 3 else 0)
        for c in range(nchunks):
            mod_step()
            n = 128 if c < 8 else NS
            ti = min(c // 4, 2)
            toff = c * 128
            pv = [PS(), PS()]
            for p in range(4):
                bank, bankb = pv[p // 2]
                B.emit("pe", [mm(bank[0:n, (p % 2) * PW:(p % 2 + 1) * PW], H[:, kc, toff:toff + n], vsl[p][0][:, kc, :],
                                 kc == 0, kc == 7) for kc in range(8)], reads=[Hb[ti], vsl[p][1]], writes=[bankb])
            st, stb = T2K()
            for b2 in range(2):
                junk, junkb = T2K()
                B.emit("act", act(junk[0:n, :], pv[b2][0][0:n, :], AF.Square, accum_out=st[0:n, b2:b2 + 1]),
                       reads=[pv[b2][1]], writes=[junkb, stb])
            B.emit("dve", tt(st[0:n, 2:3], st[0:n, 0:1], st[0:n, 1:2], ALU.add), reads=[stb], writes=[stb])
            B.emit("act", act(st[0:n, 3:4], st[0:n, 2:3], AF.Ln, scale=1.0 / D, bias=SMALL[0:n, 0:1]),
                   reads=[stb, epsb], writes=[stb])
            B.emit("act", act(st[0:n, 4:5], st[0:n, 3:4], AF.Exp, scale=-0.5), reads=[stb], writes=[stb])
            for b2 in range(2):
                B.emit("dve", stt(Vv[0:n, c, b2 * 512:(b2 + 1) * 512], pv[b2][0][0:n, :], st[0:n, 4:5],
                                  GB[0:n, b2 * 512:(b2 + 1) * 512], ALU.mult, ALU.mult),
                       reads=[pv[b2][1], stb, GBb], writes=[Vb[c]])
                if c == 8:
                    vs, vsb = T2K()
                    B.emit("dve", stt(vs[0:n, :], pv[b2][0][0:n, :], st[0:n, 4:5], GB[0:n, b2 * 512:(b2 + 1) * 512],
                                      ALU.mult, ALU.mult), reads=[pv[b2][1], stb, GBb], writes=[vsb])
                    B.dma("sp", lambda h, vs=vs, b2=b2: h.dma_start(out=cv_d[l, :, b2 * 512:(b2 + 1) * 512],
                                                                   in_=vs[0:NS, :]), ods_for(vsb), reads=[vsb])

        release_piece(*[v_[1] for v_ in vsl])
        pieces_B = {}

        def get_pieces_B(jp):
            if jp not in pieces_B:
                pieces_B[jp] = (load_piece("w_in", l, 3 * D + jp * PW), load_piece("w_in", l, 4 * D + jp * PW))
            return pieces_B[jp]

        def build_DG(j):
            for k in range(4):
                B.emit("dve", ts(DG[:, j % 2, k, :], IDENT[:, :], P(j, R_CW + k, l), None, ALU.mult),
                       reads=[conb2, PRMb], writes=[DGb[j % 2]])

        def gen_S1(j):
            rb = j % 2
            (slx, slxb), _ = get_pieces_B(j // 2)
            s = j % 2
            dgi = j % 2
            if j == 0:
                build_DG(0)
            if hf == 0:
                B.emit("dve", lambda h: h.memset(XBv[:, 0:3], 0.0), writes=[XBb])
            else:
                B.emit("dve", cp(XBv[:, 0:3], CXB[:, j, :]), reads=[CXBb], writes=[XBb])
            for ti, (off, W) in enumerate(tiles):
                px, pxb = PS()
                B.emit("pe", [mm(px[:, 0:W], slx[:, kc, s * 128:(s + 1) * 128], H[:, kc, off:off + W], kc == 0,
                                 kc == 7) for kc in range(8)], reads=[slxb, Hb[ti]], writes=[pxb])
                yield
                if ti == 1 and j + 1 < 8:
                    build_DG(j + 1)
                pcv, pcvb = PS()
                if W == 512:
                    B.emit("act", act(XBv[:, 3 + off:3 + off + W], px[:, 0:W], AF.Copy), reads=[pxb], writes=[XBb])
                    if hf == 1 and ti == 1:
                        B.emit("dve", cp(CPT[:, j, :], px[:, W - 3:W]), reads=[pxb, XBb], writes=[CPTb])
                    if hf == 0 and ti == 1:
                        B.emit("dve", cp(CXB[:, j, :], px[:, W - 3:W]), reads=[pxb, XBb], writes=[CXBb])
                    B.emit("pe", [mm(pcv[:, 0:W], DG[:, dgi, k, :], XBv[:, off + k:off + k + W], k == 0, k == 3)
                                  for k in range(4)], reads=[DGb[dgi], XBb], writes=[pcvb])
                else:
                    B.emit("act", act(SCN[:, j, :], px[:, 0:W], AF.Copy), reads=[pxb], writes=[SCNb])
                    B.emit("dve", cp(CSS[:, j, :], px[:, 0:W]), reads=[pxb, SCNb], writes=[CSSb])
                    B.emit("pe", [mm(pcv[:, 0:W], DG[:, dgi, k, :], (SCVA[:, l, j, k, :] if k < 3 else SCN[:, j, :]),
                                     k == 0, k == 3) for k in range(4)], reads=[DGb[dgi], SCVb, SCNb], writes=[pcvb])
                yield
                xc, xcb_ = T2K()
                B.emit("act", act(xc[:, 0:W], pcv[:, 0:W], AF.Identity, bias=P(j, R_CB, l)), reads=[pcvb, PRMb],
                       writes=[xcb_])
                xh, xhb = TB()
                B.emit("act", act(xh[:, 0:W], pcv[:, 0:W], AF.Identity, bias=P(j, R_CB, l)), reads=[pcvb, PRMb],
                       writes=[xhb])
                pr, prb = PS()
                B.emit("pe", mm(pr[:, 0:W], RG[:, j, 0, :], xh[:, 0:W], True, True), reads=[RGb, xhb], writes=[prb])
                pi_, pib = PS()
                B.emit("pe", mm(pi_[:, 0:W], RG[:, j, 1, :], xh[:, 0:W], True, True), reads=[RGb, xhb], writes=[pib])
                yield
                trr, trb = T2K()
                B.emit("act", act(trr[:, 0:W], pr[:, 0:W], AF.Tanh, scale=0.5, bias=HB[:, 0, l, j:j + 1]),
                       reads=[prb, NSPb], writes=[trb])
                B.emit("act", act(ROW_S[:, off:off + W], trr[:, 0:W], AF.Exp, scale=NSP[:, 1, l, j:j + 1],
                                  bias=NSP[:, 1, l, j:j + 1]), reads=[trb, NSPb], writes=[rowSb[ti]])
                B.emit("act", act(ROW_A[rb][:, off:off + W], trr[:, 0:W], AF.Exp, scale=NSP[:, 0, l, j:j + 1],
                                  bias=NSP[:, 0, l, j:j + 1]), reads=[trb, NSPb], writes=[rowAb[rb][ti]])
                tii, tib = T2K()
                B.emit("act", act(tii[:, 0:W], pi_[:, 0:W], AF.Tanh, scale=0.5, bias=HB[:, 1, l, j:j + 1]),
                       reads=[pib, NSPb], writes=[tib])
                B.emit("dve", stt(ROW_P[rb][:, off:off + W], tii[:, 0:W], 1.0, xc[:, 0:W], ALU.add, ALU.mult),
                       reads=[tib, xcb_], writes=[rowPb[rb][ti]])
            if j % 2 == 1:
                release_piece(slxb)

        def emit_S2(j):
            rb = j % 2
            for ti, (off, W) in enumerate(tiles):
                B.emit("act", act(ROW_S[:, off:off + W], ROW_S[:, off:off + W], AF.Sqrt, scale=-0.25,
                                  bias=SMALL[:, 1:2]), reads=[rowSb[ti], epsb], writes=[rowSb[ti]])
            for ti, (off, W) in enumerate(tiles):
                B.emit("dve", tt(ROW_P[rb][:, off:off + W], ROW_P[rb][:, off:off + W], ROW_S[:, off:off + W],
                                 ALU.mult), reads=[rowPb[rb][ti], rowSb[ti]], writes=[rowPb[rb][ti]])

        def gen_S3(j):
            rb = j % 2
            _, (slg, slgb) = get_pieces_B(j // 2)
            s = j % 2
            for ti, (off, W) in enumerate(tiles):
                hs_, hsb = T2K()
                if W == 512:
                    if hf == 0 and ti == 0:
                        init = 0.0
                        rd = []
                    else:
                        init = HC[:, j:j + 1]
                        rd = [HCb]
                    B.emit("dve", lambda h, hs_=hs_, off=off, W=W, init=init, rb=rb: h.tensor_tensor_scan(
                        out=hs_[:, 0:W], data0=ROW_A[rb][:, off:off + W], data1=ROW_P[rb][:, off:off + W],
                        initial=init, op0=ALU.mult, op1=ALU.add), reads=[rowAb[rb][ti], rowPb[rb][ti]] + rd,
                        writes=[hsb])
                    if hf == 1 and ti == 1:
                        B.emit("dve", cp(HP[:, j:j + 1], hs_[:, W - 1:W]), reads=[hsb], writes=[HPb])
                    else:
                        B.emit("dve", cp(HC[:, j:j + 1], hs_[:, W - 1:W]), reads=[hsb], writes=[HCb])
                else:
                    B.emit("dve", tt(hs_[:, 0:W], ROW_A[rb][:, off:off + W], H0SA[:, l, j, :], ALU.mult),
                           reads=[rowAb[rb][ti], H0Sb], writes=[hsb])
                    B.emit("dve", tt(hs_[:, 0:W], hs_[:, 0:W], ROW_P[rb][:, off:off + W], ALU.add),
                           reads=[hsb, rowPb[rb][ti]], writes=[hsb])
                    B.emit("dve", cp(HSS[:, j, :], hs_[:, 0:W]), reads=[hsb], writes=[HSSb])
                pg, pgb = PS()
                B.emit("pe", [mm(pg[:, 0:W], slg[:, kc, s * 128:(s + 1) * 128], H[:, kc, off:off + W], kc == 0,
                                 kc == 7) for kc in range(8)], reads=[slgb, Hb[ti]], writes=[pgb])
                yield
                tg, tgb = T2K()
                B.emit("act", act(tg[:, 0:W], pg[:, 0:W], AF.Tanh, scale=0.5), reads=[pgb], writes=[tgb])
                B.emit("dve", stt(tg[:, 0:W], tg[:, 0:W], 1.0, pg[:, 0:W], ALU.add, ALU.mult), reads=[tgb, pgb],
                       writes=[tgb])
                B.emit("dve", stt(YB[:, j, off:off + W], hs_[:, 0:W], 0.5, tg[:, 0:W], ALU.mult, ALU.mult),
                       reads=[hsb, tgb], writes=[YBb[j][ti]])
            if j % 2 == 1:
                release_piece(slgb)

        def gen_B():
            yield from gen_S1(0)
            emit_S2(0)
            yield 2
            for j in range(8):
                g3 = gen_S3(j)
                g1 = gen_S1(j + 1) if j + 1 < 8 else iter(())
                a3 = a1 = True
                while a3 or a1:
                    if a1:
                        try:
                            next(g1)
                            yield
                        except StopIteration:
                            a1 = False
                    if a3:
                        try:
                            next(g3)
                            yield
                        except StopIteration:
                            a3 = False
                if j + 1 < 8:
                    emit_S2(j + 1)
                    yield 2

        def gen_A():
            for gp in range(4):
                slu, slub = load_piece("w_in", l, 0 * D + gp * PW)
                sla, slab = load_piece("w_in", l, 2 * D + gp * PW)
                for s in range(2):
                    g = gp * 2 + s
                    for ti, (off, W) in enumerate(tiles):
                        pu, pub = PS(pin=True)
                        B.emit("pe", [mm(pu[:, 0:W], slu[:, kc, s * 128:(s + 1) * 128], H[:, kc, off:off + W], kc == 0,
                                         kc == 7) for kc in range(8)], reads=[slub, Hb[ti]], writes=[pub])
                        yield
                        pga, pgab = PS(pin=True)
                        B.emit("pe", [mm(pga[:, 0:W], sla[:, kc, s * 128:(s + 1) * 128], H[:, kc, off:off + W], kc == 0,
                                         kc == 7) for kc in range(8)], reads=[slab, Hb[ti]], writes=[pgab])
                        yield
                        pS, pSb = PS(pin=True)
                        if W == 512:
                            fns = [mm(pS[:, 0:512], ONESB[0:2, :], BS2[0:2, g:g + 1, :].to_broadcast([2, 4, 128]),
                                      True, False)]
                            rds = [WSTb, BSb, BSLb, conb2]
                            for q in range(4):
                                c = (off // 128) + q
                                fns.append(mm(pS[:, q * 128:(q + 1) * 128], Vv[:, c, g * 128:(g + 1) * 128],
                                              WST[:, g, :], False, q == 3))
                                rds.append(Vb[c])
                            B.emit("pe", fns, reads=rds, writes=[pSb])
                        else:
                            o = pS[:, 0:NS]
                            fns = [mm(o, ONESB[0:2, :], BS2S[0:2, g, :], True, False),
                                   mm(o, Vv[0:NS, 8, g * 128:(g + 1) * 128], WSTS[:, g, :], False, True)]
                            B.emit("pe", fns, reads=[WSTSb, BSb, BSLb, conb2, Vb[8]], writes=[pSb])
                        tg, tgb = T2K()
                        B.emit("act", act(tg[:, 0:W], pga[:, 0:W], AF.Tanh, scale=0.5), reads=[pgab], writes=[tgb])
                        B.emit("dve", stt(tg[:, 0:W], tg[:, 0:W], 1.0, pga[:, 0:W], ALU.add, ALU.mult), reads=[tgb, pgab],
                               writes=[tgb])
                        B.emit("dve", tt(tg[:, 0:W], pu[:, 0:W], tg[:, 0:W], ALU.mult), reads=[pub, tgb], writes=[tgb])
                        B.emit("dve", stt(YA[:, g, off:off + W], tg[:, 0:W], 0.5, pS[:, 0:W], ALU.mult, ALU.mult),
                               reads=[tgb, pSb], writes=[YAb[g][ti]])
                        PS_release(pub, pgab, pSb)
                        yield
                    if s == 1 and ti == nt - 1:
                        release_piece(slub, slab)


        B.mark(f'BA{l}{hf}')
        gens = [gen_B(), gen_A()]
        alive = [True, True]
        n_pro = 3 * nt
        rate = (18.0 * nt - 16.0) / (29.0 * nt)
        credit = 0.0
        bstep = 0
        while alive[0] or alive[1]:
            extra = None
            if alive[0]:
                try:
                    extra = next(gens[0])
                except StopIteration:
                    alive[0] = False
            if not alive[0]:
                credit += 1.0
            elif extra:
                credit += extra
            else:
                bstep += 1
                credit += 2.0 if bstep <= n_pro else rate
            while credit >= 1.0 and alive[1]:
                credit -= 1.0
                try:
                    next(gens[1])
                except StopIteration:
                    alive[1] = False
            if not alive[1]:
                credit = 0.0

        B.mark(f'M{l}{hf}')
        regVM.handoff()
        for np_ in range(4):
            mod_step()
            sza, szab = load_piece("w_in", l, 5 * D + np_ * PW)
            spa, spab = load_piece("w_pa", l, np_ * PW)
            szb, szbb = load_piece("w_in", l, 6 * D + np_ * PW)
            spb, spbb = load_piece("w_pb", l, np_ * PW)
            for s in range(2):
                n_ = np_ * 2 + s
                cs = slice(s * 128, (s + 1) * 128)
                for ti, (off, W) in enumerate(tiles):
                    res = []
                    for (sz, szb_, sp_, spb_, Y, Yb) in ((sza, szab, spa, spab, YA, YAb), (szb, szbb, spb, spbb, YB, YBb)):
                        pz, pzb = PS()
                        B.emit("pe", [mm(pz[:, 0:W], sz[:, kc, cs], H[:, kc, off:off + W], kc == 0, kc == 7)
                                      for kc in range(8)], reads=[szb_, Hb[ti]], writes=[pzb])
                        pp_, ppb_ = PS()
                        B.emit("pe", [mm(pp_[:, 0:W], sp_[:, kc, cs], Y[:, kc, off:off + W], kc == 0, kc == 7)
                                      for kc in range(8)], reads=[spb_] + [Yb[kc][ti] for kc in range(8)], writes=[ppb_])
                        tz, tzb = T2K()
                        B.emit("act", act(tz[:, 0:W], pz[:, 0:W], AF.Tanh, scale=0.5), reads=[pzb], writes=[tzb])
                        B.emit("dve", stt(tz[:, 0:W], tz[:, 0:W], 1.0, pp_[:, 0:W], ALU.add, ALU.mult),
                               reads=[tzb, ppb_], writes=[tzb])
                        res.append((tz, tzb))
                    B.emit("dve", tt(Mv[:, n_, off:off + W], res[0][0][:, 0:W], res[1][0][:, 0:W], ALU.add),
                           reads=[res[0][1], res[1][1]], writes=[Mb[ti]])
            release_piece(szab, spab, szbb, spbb)

        if l + 1 < DEPTH:
            layer_setup_dma(l + 1, hf)
        B.mark(f'O{l}{hf}')
        for np_ in range(4):
            so, sob = load_piece("w_out", l, np_ * PW)
            for s in range(2):
                n_ = np_ * 2 + s
                for ti, (off, W) in enumerate(tiles):
                    xt = xtile(hf, ti)
                    po, pob = PS()
                    B.emit("pe", [mm(po[:, 0:W], so[:, kc, s * 128:(s + 1) * 128], Mv[:, kc, off:off + W], kc == 0,
                                     kc == 7) for kc in range(8)], reads=[sob, Mb[ti]], writes=[pob])
                    xa = X[:, n_, c0 + off:c0 + off + W]
                    if W == 512:
                        B.emit("dve", stt(xa, po[:, 0:W], DER[m][:, 1, n_:n_ + 1], xa, ALU.mult, ALU.add),
                               reads=[pob, DERb[m], Xb[n_][xt]], writes=[Xb[n_][xt]])
                    else:
                        tmp, tmpb = OTMP, OTMPb
                        B.emit("dve", tt(tmp[:, 0:W], po[:, 0:W], DERS[m][:, 1, n_, :], ALU.mult), reads=[pob, DERSb],
                               writes=[tmpb])
                        B.emit("dve", tt(xa, xa, tmp[:, 0:W], ALU.add), reads=[tmpb, Xb[n_][xt]], writes=[Xb[n_][xt]])
            release_piece(sob)

    def fm16_to_rows(src, srcb, dram_rows, name):
        for hh in range(2):
            p_, pb_ = PS()
            fns = [tr(p_[0:NS, q * 128:(q + 1) * 128], src[:, hh * 4 + q, :], IDENT[:, :]) for q in range(4)]
            B.emit("pe", fns, reads=[srcb, conb2], writes=[pb_])
            t, tbuf = T2K()
            B.emit("act", act(t[0:NS, :], p_[0:NS, :], AF.Copy), reads=[pb_], writes=[tbuf])
            B.dma("sp", lambda h, t=t, hh=hh: h.dma_start(out=dram_rows[:, hh * 512:(hh + 1) * 512], in_=t[0:NS, :]),
                  ods_for(tbuf), reads=[tbuf])

    def layer_outputs(l):
        B.mark(f'out{l}')
        B.dma("sp", lambda h: h.dma_start(out=hp_d[l].rearrange("(j p) -> p j", p=128), in_=HP[:, :],
                                          allow_slow_non_contiguous=True), ods_for(HPb), reads=[HPb])
        for k in range(3):
            B.dma("sp", lambda h, k=k: h.dma_start(out=cp_d[l, k].rearrange("(j p) -> p j", p=128), in_=CPT[:, :, k],
                                                   allow_slow_non_contiguous=True), ods_for(CPTb), reads=[CPTb])
        fm16_to_rows(HSS, HSSb, hs_d[l], f"hs{l}")
        fm16_to_rows(CSS, CSSb, cs_d[l, :, 2, :], f"cs2_{l}")

    def final_out(hf):
        B.mark(f'final{hf}')
        B.dma("sp", lambda h: h.dma_start(out=GB[:], in_=final_g_d[0:1, :].to_broadcast([128, D])), mds("gb"),
              writes=[GBb])
        nch = 8 + (1 if hf == 1 else 0)
        for c in range(nch):
            n = 128 if c < 8 else NS
            col0 = c * 128
            xt = min(col0 // 512, 2)
            pv = [PS(), PS()]
            for hh in range(2):
                fns = [tr(pv[hh][0][0:n, q * 128:(q + 1) * 128], X[:, hh * 4 + q, col0:col0 + n], IDENT[:, :])
                       for q in range(4)]
                B.emit("pe", fns, reads=[Xb[hh * 4 + q][xt] for q in range(4)] + [conb2], writes=[pv[hh][1]])
            st, stb = T2K()
            for b2 in range(2):
                junk, junkb = T2K()
                B.emit("act", act(junk[0:n, :], pv[b2][0][0:n, :], AF.Square, accum_out=st[0:n, b2:b2 + 1]),
                       reads=[pv[b2][1]], writes=[junkb, stb])
            B.emit("dve", tt(st[0:n, 2:3], st[0:n, 0:1], st[0:n, 1:2], ALU.add), reads=[stb], writes=[stb])
            B.emit("act", act(st[0:n, 3:4], st[0:n, 2:3], AF.Sqrt, scale=1.0 / D, bias=SMALL[0:n, 0:1]),
                   reads=[stb, epsb], writes=[stb])
            B.emit("dve", lambda h, st=st, n=n: h.reciprocal(out=st[0:n, 4:5], in_=st[0:n, 3:4]), reads=[stb],
                   writes=[stb])
            for b2 in range(2):
                o, ob = T2K()
                B.emit("dve", stt(o[0:n, :], pv[b2][0][0:n, :], st[0:n, 4:5], GB[0:n, b2 * 512:(b2 + 1) * 512],
                                  ALU.mult, ALU.mult), reads=[pv[b2][1], stb, GBb], writes=[ob])
                if c < 8:
                    r0 = (hf * 8 + c) * 128
                    dst = yp_d[r0:r0 + 128, b2 * 512:(b2 + 1) * 512]
                else:
                    dst = ys_d[:, b2 * 512:(b2 + 1) * 512]
                B.dma("sp", lambda h, o=o, n=n, dst=dst: h.dma_start(out=dst, in_=o[0:n, :]), ods_for(ob),
                      reads=[ob])

    epsb = Buf("eps", track=False)
    B.emit("dve", lambda h: h.memset(SMALL[:, 0:1], EPS), reads=[SMALLb, t1b], writes=[SMALLb])
    B.emit("dve", lambda h: h.memset(SMALL[:, 1:2], 0.25), writes=[SMALLb, epsb])

    load_x(0)
    for _ in emit_mod(0):
        pass
    for hf in range(2):
        if hf == 1:
            load_sample_state()
            load_x(1)
        for l in range(DEPTH):
            if hf == 1:
                emit_ders(l)
            mg = emit_mod(l + 1) if (hf == 0 and l + 1 < DEPTH) else None
            half_layer(l, hf, mg)
            if mg is not None:
                for _ in mg:
                    pass
            if hf == 1:
                layer_outputs(l)
        final_out(hf)

    fin = []
    for d in out_ds:
        if d.n > 0:
            fin.append((d.sem, 16 * d.n))
    B.ops["sp"].append((fin, [], None))

    block = B.es.enter_context(nc.Block())

    @block.tensor
    def _(h):
        B.replay("pe", h)

    @block.scalar
    def _(h):
        B.replay("act", h)

    @block.vector
    def _(h):
        B.replay("dve", h)

    @block.gpsimd
    def _(h):
        B.replay("pool", h)

    @block.sync
    def _(h):
        B.replay("sp", h)

    B.es.close()
    if _osx.environ.get("KMARK"):
        import json as _json
        _json.dump(B.marks, open(_osx.environ["KMARK"], "w"))
    return nc


_NC_CACHE = {}


def kernel(x_prompt, x_sample, c_prompt, c_sample, state_rglru_h, state_conv,
           w_ada, b_ada, norm_g, w_in, v_norm_g, w_s, b_s, conv_w, conv_b,
           w_rg_a, b_rg_a, w_rg_x, b_rg_x, lam, w_pa, w_pb, w_out, final_g):
    f = lambda a: np.ascontiguousarray(np.asarray(a), dtype=np.float32)
    if "nc" not in _NC_CACHE:
        _NC_CACHE["nc"] = build_program()
    nc = _NC_CACHE["nc"]
    ident = np.eye(128, dtype=np.float32)
    mask = np.triu(np.ones((128, 128), dtype=np.float32))
    shared = {
        "w_ada": f(w_ada), "b_ada": f(b_ada), "norm_g": f(norm_g), "w_in": f(w_in), "v_norm_g": f(v_norm_g),
        "w_s": f(w_s), "b_s": f(b_s), "conv_w": f(conv_w), "conv_b": f(conv_b), "w_rg_a": f(w_rg_a),
        "b_rg_a": f(b_rg_a), "w_rg_x": f(w_rg_x), "b_rg_x": f(b_rg_x), "lam": f(lam), "w_pa": f(w_pa),
        "w_pb": f(w_pb), "w_out": f(w_out), "final_g": f(final_g).reshape(1, D), "ident": ident, "mask": mask,
    }
    x_prompt = f(x_prompt)
    x_sample = f(x_sample)
    c_prompt = f(c_prompt)
    c_sample = f(c_sample)
    state_rglru_h = f(state_rglru_h)
    state_conv = f(state_conv)
    in_maps = []
    for c in range(NCORES):
        r = slice(c * NS, (c + 1) * NS)
        m = dict(shared)
        m["xp"] = np.ascontiguousarray(x_prompt[c])
        m["xs"] = np.ascontiguousarray(x_sample[r, 0, :])
        m["c17"] = np.ascontiguousarray(np.concatenate([c_prompt[c:c + 1], c_sample[r]], axis=0))
        m["sh"] = np.ascontiguousarray(state_rglru_h[:, r, :])
        m["sc"] = np.ascontiguousarray(state_conv[:, r, :, :])
        in_maps.append(m)
    res = run_bass_kernel_spmd(nc, in_maps, core_ids=list(range(NCORES)))
    R = res.results
    y_prompt = np.stack([R[c]["yp"] for c in range(NCORES)], axis=0)
    y_sample = np.concatenate([R[c]["ys"] for c in range(NCORES)], axis=0)[:, None, :]
    h_prompt = np.stack([R[c]["hp"] for c in range(NCORES)], axis=1)
    conv_prompt = np.stack([R[c]["cp"] for c in range(NCORES)], axis=1)
    h_sample = np.concatenate([R[c]["hs"] for c in range(NCORES)], axis=1)
    conv_sample = np.concatenate([R[c]["cs"] for c in range(NCORES)], axis=1)
    chunk_v = np.concatenate([R[c]["cv"] for c in range(NCORES)], axis=1)[:, :, None, :]
    return (y_prompt.astype(np.float32), y_sample.astype(np.float32), h_prompt.astype(np.float32),
            conv_prompt.astype(np.float32), h_sample.astype(np.float32), conv_sample.astype(np.float32),
            chunk_v.astype(np.float32))
```

```python
import numpy as np
from contextlib import ExitStack
import concourse.bass as bass
import concourse.mybir as mybir
from concourse.bass_utils import run_bass_kernel_spmd

F32 = mybir.dt.float32
BF16 = mybir.dt.bfloat16
ALU = mybir.AluOpType
AF = mybir.ActivationFunctionType

NCORES = 8
D = 1024
SEQ = 2048
NS = 16
DEPTH = 4
TT = SEQ + NS
THM = 1040
EPS = 1e-6
NPR = 12
PW = 256
NSLOT = 8
NT2K = 9
NTB = 3


class Buf:
    __slots__ = ("name", "w", "r", "region", "track", "psum")

    def __init__(self, name, region=None, track=True, psum=False):
        self.psum = psum
        self.name = name
        self.w = None
        self.r = []
        self.region = region
        self.track = track


class Region:
    def __init__(self):
        self.fence = []
        self.bufs = []

    def new(self, name):
        b = Buf(name, region=self)
        self.bufs.append(b)
        return b

    def handoff(self):
        deps = list(self.fence)
        for b in self.bufs:
            if b.w is not None:
                deps.append(b.w)
            deps.extend(b.r)
            b.w = None
            b.r = []
        best = {}
        for s, v in deps:
            k = id(s)
            if k not in best or best[k][1] < v:
                best[k] = (s, v)
        self.fence = list(best.values())


class DSem:
    def __init__(self, sem):
        self.sem = sem
        self.n = 0


ENGS = ("pe", "act", "dve", "pool", "sp")


class Builder:
    def __init__(self):
        self.nc = bass.Bass("TRN2", target_bir_lowering=False)
        self.es = ExitStack()
        self.ops = {e: [] for e in ENGS}
        self.cnt = {e: 0 for e in ENGS}
        self.seen = {e: {} for e in ENGS}
        self.esem = {}
        for e in ("pe", "act", "dve", "pool"):
            self.esem[e] = self.es.enter_context(self.nc.semaphore("sem_" + e))
        self.out_dsems = []
        self.nsem = 4

    def sb(self, name, shape, dt):
        return self.es.enter_context(self.nc.sbuf_tensor(name, shape, dt))

    def dsem(self, name):
        self.nsem += 1
        return DSem(self.es.enter_context(self.nc.semaphore(name)))

    def _waits(self, eng, reads, writes):
        deps = {}

        def add(d, raw):
            s, v = d
            k = id(s)
            if k not in deps or deps[k][1] < v:
                deps[k] = (s, v)

        for b in reads:
            if b.w is not None:
                add(b.w, True)
            if b.psum:
                for d in b.r:
                    add(d, False)
        for b in writes:
            if b.w is not None:
                add(b.w, False)
            for d in b.r:
                add(d, False)
            if b.region is not None:
                for d in b.region.fence:
                    add(d, False)
        waits = []
        seen = self.seen[eng]
        for k, (s, v) in deps.items():
            if seen.get(k, 0) >= v:
                continue
            seen[k] = v
            waits.append((s, v))
        return waits

    def emit(self, eng, fns, reads=(), writes=()):
        if not isinstance(fns, (list, tuple)):
            fns = [fns]
        waits = self._waits(eng, reads, writes)
        self.cnt[eng] += 1
        dep = (self.esem[eng], self.cnt[eng])
        for b in reads:
            if b.track:
                b.r.append(dep)
        for b in writes:
            b.w = dep
            b.r = []
        self.ops[eng].append((waits, list(fns), (self.esem[eng], 1)))

    def dma(self, queue, fn, ds, reads=(), writes=()):
        waits = self._waits(queue, reads, writes)
        ds.n += 1
        dep = (ds.sem, 16 * ds.n)
        for b in reads:
            if b.track:
                b.r.append(dep)
        for b in writes:
            b.w = dep
            b.r = []
        self.ops[queue].append((waits, [fn], (ds.sem, 16)))

    def mark(self, name):
        if not hasattr(self, "marks"):
            self.marks = []
        self.marks.append((name, {e: sum(len(o[1]) for o in self.ops[e]) for e in ENGS}))

    def replay(self, eng, h):
        for waits, fns, inc in self.ops[eng]:
            for s, v in waits:
                h.wait_ge(s, v)
            n = len(fns)
            for i, f in enumerate(fns):
                ins = f(h)
                if i == n - 1 and inc is not None:
                    ins.then_inc(inc[0], inc[1])


import os as _osx


def build_program():
    B = Builder()
    nc = B.nc
    sb = B.sb

    def din(name, shape):
        return nc.dram_tensor(name, shape, F32, kind="ExternalInput").ap()

    def dout(name, shape):
        return nc.dram_tensor(name, shape, F32, kind="ExternalOutput").ap()

    xp_d = din("xp", [SEQ, D])
    xs_d = din("xs", [NS, D])
    c17_d = din("c17", [17, D])
    sh_d = din("sh", [DEPTH, NS, D])
    sc_d = din("sc", [DEPTH, NS, 3, D])
    w_ada_d = din("w_ada", [DEPTH, D, 3 * D])
    b_ada_d = din("b_ada", [DEPTH, 3 * D])
    norm_g_d = din("norm_g", [DEPTH, D])
    w_in_d = din("w_in", [DEPTH, D, 7 * D])
    v_norm_g_d = din("v_norm_g", [DEPTH, D])
    w_s_d = din("w_s", [DEPTH, 8, 128, 128])
    b_s_d = din("b_s", [DEPTH, 8, 128])
    conv_w_d = din("conv_w", [DEPTH, 4, D])
    conv_b_d = din("conv_b", [DEPTH, D])
    w_rg_a_d = din("w_rg_a", [DEPTH, 16, 64, 64])
    b_rg_a_d = din("b_rg_a", [DEPTH, D])
    w_rg_x_d = din("w_rg_x", [DEPTH, 16, 64, 64])
    b_rg_x_d = din("b_rg_x", [DEPTH, D])
    lam_d = din("lam", [DEPTH, D])
    w_pa_d = din("w_pa", [DEPTH, D, D])
    w_pb_d = din("w_pb", [DEPTH, D, D])
    w_out_d = din("w_out", [DEPTH, D, D])
    final_g_d = din("final_g", [1, D])
    ident_d = din("ident", [128, 128])
    mask_d = din("mask", [128, 128])

    yp_d = dout("yp", [SEQ, D])
    ys_d = dout("ys", [NS, D])
    hp_d = dout("hp", [DEPTH, D])
    cp_d = dout("cp", [DEPTH, 3, D])
    hs_d = dout("hs", [DEPTH, NS, D])
    cs_d = dout("cs", [DEPTH, NS, 3, D])
    cv_d = dout("cv", [DEPTH, NS, D])

    X = sb("X", [128, 8, THM], F32)
    Xb = [[Buf(f"X{j}_{t}") for t in range(3)] for j in range(8)]
    H = sb("H", [128, 8, THM], BF16)
    Hb = [Buf(f"H{t}") for t in range(3)]
    regVM = Region()
    VM = sb("VM", [128, 9 * 1024], BF16)
    Vv = VM[:, :].rearrange("p (c f) -> p c f", c=9)
    Vb = [regVM.new(f"V{c}") for c in range(9)]
    Mv = VM[:, 0:8 * THM].rearrange("p (j t) -> p j t", j=8)
    Mb = [regVM.new(f"M{t}") for t in range(3)]
    ROWS = sb("ROWS", [128, 5, THM], F32)
    ROW_A = [ROWS[:, 0, :], ROWS[:, 1, :]]
    ROW_S = ROWS[:, 2, :]
    ROW_P = [ROWS[:, 3, :], ROWS[:, 4, :]]
    rowAb = [[Buf(f"rowA{r}_{t}") for t in range(3)] for r in range(2)]
    rowSb = [Buf(f"rowS{t}") for t in range(3)]
    rowPb = [[Buf(f"rowP{r}_{t}") for t in range(3)] for r in range(2)]
    XBv = sb("XB", [128, 3 + THM + 1], BF16)
    XBb = Buf("XB")
    YA = sb("YA", [128, 8, THM], BF16)
    YAb = [[Buf(f"YA{j}_{t}") for t in range(3)] for j in range(8)]
    YB = sb("YB", [128, 8, THM], BF16)
    YBb = [[Buf(f"YB{j}_{t}") for t in range(3)] for j in range(8)]

    slots = [sb(f"slot{i}", [128, 8, PW], BF16) for i in range(NSLOT)]
    slotb = [Buf(f"slot{i}") for i in range(NSLOT)]
    slotsem = [B.dsem(f"slotsem{i}") for i in range(NSLOT)]
    slot_ctr = [0]

    t2k = [sb(f"t2k{i}", [128, 512], F32) for i in range(NT2K)]
    t2kb = [Buf(f"t2k{i}") for i in range(NT2K)]
    t2k_ctr = [0]
    tb = [sb(f"tb{i}", [128, 512], BF16) for i in range(NTB)]
    tbb = [Buf(f"tb{i}") for i in range(NTB)]
    tb_ctr = [0]

    ps = [B.es.enter_context(nc.psum_tensor(f"ps{i}", [128, 512], F32)) for i in range(8)]
    psb = [Buf(f"ps{i}", psum=True) for i in range(8)]
    ps_ctr = [0]

    def T2K():
        i = t2k_ctr[0] % NT2K
        t2k_ctr[0] += 1
        return t2k[i], t2kb[i]

    def TB():
        i = tb_ctr[0] % NTB
        tb_ctr[0] += 1
        return tb[i], tbb[i]

    def PS():
        i = ps_ctr[0] % 7
        ps_ctr[0] += 1
        return ps[i], psb[i]

    GB = sb("GB", [128, D], F32)
    GBb = Buf("GB")
    PRM = sb("PRM", [128, 8, NPR * DEPTH], F32)
    PRMb = Buf("PRM", track=False)
    IDENT = sb("IDENT", [128, 128], F32)
    IDENTB = sb("IDENTB", [128, 128], BF16)
    MASK = sb("MASK", [128, 128], F32)
    ONESB = sb("ONESB", [128, 128], BF16)
    CONb = Buf("consts", track=False)
    NSP = sb("NSP", [128, 2, DEPTH, 8], F32)
    HB = sb("HB", [128, 2, DEPTH, 8], F32)
    NSPb = Buf("NSP", track=False)
    SMALL = sb("SMALL", [128, 64], F32)
    SMALLb = Buf("SMALL")
    CT = sb("CT", [128, 8, 17], BF16)
    CTb = Buf("CT", track=False)
    MOD = [sb(f"MOD{i}", [128, 3, 8, 17], F32) for i in range(DEPTH)]
    MODb = [Buf(f"MOD{i}", track=False) for i in range(DEPTH)]
    DER = [sb(f"DER{i}", [128, 2, 8], F32) for i in range(DEPTH)]
    DERS1 = sb("DERS", [128, 2, 8, NS], F32)
    DERS = [DERS1 for i in range(DEPTH)]
    DERSb = Buf("DERS")
    DERb = [Buf(f"DER{i}", track=False) for i in range(DEPTH)]
    RG = sb("RG", [128, 8, 2, 128], BF16)
    RGb = Buf("RG")
    DG = sb("DG", [128, 2, 4, 128], BF16)
    DGb = [Buf("DG0"), Buf("DG1")]
    dg_ctr = [0]
    WST = sb("WST", [128, 8, 128], BF16)
    WSTb = Buf("WST")
    WSTS = sb("WSTS", [16, 8, 16], BF16)
    W00 = sb("W00", [16, 8], F32)
    WSTSb = Buf("WSTS")
    BS2 = sb("BS2", [2, 8, 128], BF16)
    BS2S = sb("BS2S", [2, 8, NS], BF16)
    BSb = Buf("BShi")
    BSLb = Buf("BSlo")
    SCVA = sb("SCVA", [128, DEPTH, 8, 3, NS], BF16)
    SCN = sb("SCN", [128, 8, NS], BF16)
    SCVb = Buf("SCV", track=False)
    SCNb = Buf("SCN")
    H0SA = sb("H0SA", [128, DEPTH, 8, NS], F32)
    H0Sb = Buf("H0S", track=False)
    HSS = sb("HSS", [128, 8, NS], F32)
    HSSb = Buf("HSS")
    OTMP = sb("OTMP", [128, NS], F32)
    OTMPb = Buf("OTMP")
    CSS = sb("CSS", [128, 8, NS], F32)
    CSSb = Buf("CSS")
    HP = sb("HP", [128, 8], F32)
    HPb = Buf("HP")
    CPT = sb("CPT", [128, 8, 3], F32)
    CPTb = Buf("CPT")
    HCL = sb("HCL", [128, DEPTH, 8], F32)
    HCb = Buf("HC")
    CXBL = sb("CXBL", [128, DEPTH, 8, 3], BF16)
    CXBb = Buf("CXB")

    init_ds = B.dsem("init")
    in_ds = [B.dsem(f"in{i}") for i in range(NT2K)]
    misc_ds = {}

    def mds(name):
        if name not in misc_ds:
            misc_ds[name] = B.dsem("ds_" + name)
        return misc_ds[name]

    out_ds = []

    def ods(name):
        d = B.dsem("o_" + name)
        out_ds.append(d)
        return d

    buf_ods = {}

    def ods_for(buf):
        if id(buf) not in buf_ods:
            buf_ods[id(buf)] = ods("b_" + buf.name)
        return buf_ods[id(buf)]

    def T2K_in():
        i = t2k_ctr[0] % NT2K
        t2k_ctr[0] += 1
        return t2k[i], t2kb[i], in_ds[i]

    wview = {
        "w_in": [w_in_d[l].rearrange("(c p) n -> p c n", p=128) for l in range(DEPTH)],
        "w_pa": [w_pa_d[l].rearrange("(c p) n -> p c n", p=128) for l in range(DEPTH)],
        "w_pb": [w_pb_d[l].rearrange("(c p) n -> p c n", p=128) for l in range(DEPTH)],
        "w_out": [w_out_d[l].rearrange("(c p) n -> p c n", p=128) for l in range(DEPTH)],
        "w_ada": [w_ada_d[l].rearrange("(c p) n -> p c n", p=128) for l in range(DEPTH)],
    }

    slot_held = [False] * NSLOT

    def load_piece(which, l, c0):
        for _ in range(NSLOT):
            i = slot_ctr[0] % NSLOT
            slot_ctr[0] += 1
            if not slot_held[i]:
                break
        else:
            raise AssertionError("weight ring exhausted")
        slot_held[i] = True
        src = wview[which][l][:, :, c0:c0 + PW]
        dst = slots[i]
        B.dma("pool", lambda h, dst=dst, src=src: h.dma_start(out=dst[:], in_=src), slotsem[i],
              writes=[slotb[i]])
        return slots[i], slotb[i]

    def release_piece(*bufs):
        for b_ in bufs:
            slot_held[slotb.index(b_)] = False

    def mm(out, lhsT, rhs, start, stop):
        return lambda h: h.matmul(out, lhsT=lhsT, rhs=rhs, start=start, stop=stop)

    def tr(out, in_, ident):
        return lambda h: h.transpose(out=out, in_=in_, identity=ident)

    def act(out, in_, func, bias=None, scale=None, accum_out=None):
        kw = {}
        if bias is not None:
            kw["bias"] = bias
        if scale is not None:
            kw["scale"] = scale
        if accum_out is not None:
            kw["accum_out"] = accum_out
        return lambda h: h.activation(out=out, in_=in_, func=func, **kw)

    def stt(out, in0, scalar, in1, op0, op1):
        return lambda h: h.scalar_tensor_tensor(out=out, in0=in0, scalar=scalar, in1=in1, op0=op0, op1=op1)

    def tt(out, in0, in1, op):
        return lambda h: h.tensor_tensor(out=out, in0=in0, in1=in1, op=op)

    def ts(out, in0, s1, s2, op0, op1=None):
        if op1 is None:
            return lambda h: h.tensor_scalar(out=out, in0=in0, scalar1=s1, scalar2=None, op0=op0)
        return lambda h: h.tensor_scalar(out=out, in0=in0, scalar1=s1, scalar2=s2, op0=op0, op1=op1)

    def cp(out, in_):
        return lambda h: h.tensor_copy(out=out, in_=in_)

    B.dma("sp", lambda h: h.dma_start(out=IDENT[:], in_=ident_d), init_ds, writes=[CONb])
    B.dma("sp", lambda h: h.dma_start(out=MASK[:], in_=mask_d), init_ds, writes=[CONb])
    n_init = [2]

    def wait_init(eng):
        CONb.w = (init_ds.sem, 16 * init_ds.n)

    B.emit("dve", cp(IDENTB[:], IDENT[:]), reads=[CONb], writes=[Buf("tmp")])
    B.emit("dve", lambda h: h.memset(ONESB[:], 1.0), writes=[Buf("tmp")])
    B.emit("dve", lambda h: h.memset(RG[:].rearrange("p a b c -> p (a b c)"), 0.0), writes=[RGb])
    conb2 = Buf("consts2", track=False)
    B.emit("dve", lambda h: h.memset(SMALL[:], 0.0), writes=[SMALLb, conb2])

    prm_src = [
        lambda hh: norm_g_d[:, hh * 512:(hh + 1) * 512],
        lambda hh: conv_w_d[:, 0, hh * 512:(hh + 1) * 512],
        lambda hh: conv_w_d[:, 1, hh * 512:(hh + 1) * 512],
        lambda hh: conv_w_d[:, 2, hh * 512:(hh + 1) * 512],
        lambda hh: conv_w_d[:, 3, hh * 512:(hh + 1) * 512],
        lambda hh: conv_b_d[:, hh * 512:(hh + 1) * 512],
        lambda hh: b_rg_a_d[:, hh * 512:(hh + 1) * 512],
        lambda hh: b_rg_x_d[:, hh * 512:(hh + 1) * 512],
        lambda hh: lam_d[:, hh * 512:(hh + 1) * 512],
        lambda hh: b_ada_d[:, 0 * D + hh * 512:0 * D + (hh + 1) * 512],
        lambda hh: b_ada_d[:, 1 * D + hh * 512:1 * D + (hh + 1) * 512],
        lambda hh: b_ada_d[:, 2 * D + hh * 512:2 * D + (hh + 1) * 512],
    ]
    R_NG, R_CW, R_CB, R_BA, R_BX, R_LAM, R_ADA = 0, 1, 5, 6, 7, 8, 9
    NR = NPR * DEPTH

    def prow(k, l):
        return k * DEPTH + l

    pp, ppb = PS()
    stg = []
    for hh in range(2):
        t, tbuf, tds = T2K_in()
        for k in range(NPR):
            src = prm_src[k](hh)
            B.dma("sp", lambda h, t=t, k=k, src=src: h.dma_start(out=t[k * DEPTH:(k + 1) * DEPTH, :], in_=src),
                  tds, writes=[tbuf])
        stg.append((t, tbuf))
    for hh in range(2):
        t, tbuf = stg[hh]
        fns = [tr(pp[:, (hh * 4 + q) * NR:(hh * 4 + q + 1) * NR], t[0:NR, q * 128:(q + 1) * 128], IDENT[0:NR, 0:NR])
               for q in range(4)]
        B.emit("pe", fns, reads=[tbuf, conb2], writes=[ppb])
    B.emit("act", act(PRM[:].rearrange("p j r -> p (j r)"), pp[:, 0:8 * NR], AF.Copy), reads=[ppb], writes=[PRMb])

    def P(j, k, l):
        r = prow(k, l)
        return PRM[:, j, r:r + 1]

    lamv = PRM[:, :, prow(R_LAM, 0):prow(R_LAM, 0) + DEPTH]
    E_ = SMALL[:, 0:32].rearrange("p (j l) -> p j l", j=8)
    U_ = SMALL[:, 32:64].rearrange("p (j l) -> p j l", j=8)
    t1, t1b = T2K()
    D_ = t1[:, 0:32].rearrange("p (j l) -> p j l", j=8)
    Q_ = t1[:, 32:64].rearrange("p (j l) -> p j l", j=8)
    L_ = t1[:, 64:96].rearrange("p (j l) -> p j l", j=8)
    N_ = t1[:, 96:128].rearrange("p (j l) -> p j l", j=8)
    R_ = t1[:, 128:160].rearrange("p (j l) -> p j l", j=8)
    S_ = t1[:, 160:192].rearrange("p (j l) -> p j l", j=8)
    B.emit("act", act(E_, lamv, AF.Exp, scale=-1.0), reads=[PRMb], writes=[SMALLb])
    B.emit("dve", ts(U_, E_, 1.0, None, ALU.add), reads=[SMALLb], writes=[SMALLb])
    B.emit("dve", ts(D_, U_, -1.0, None, ALU.add), reads=[SMALLb], writes=[t1b])
    B.emit("dve", ts(Q_, D_, 0.0, None, ALU.is_equal), reads=[t1b], writes=[t1b])
    B.emit("act", act(L_, U_, AF.Ln), reads=[SMALLb, t1b], writes=[t1b])
    B.emit("dve", tt(N_, L_, Q_, ALU.add), reads=[t1b], writes=[t1b])
    B.emit("dve", tt(R_, D_, Q_, ALU.add), reads=[t1b], writes=[t1b])
    B.emit("dve", lambda h: h.reciprocal(out=R_, in_=R_), reads=[t1b], writes=[t1b])
    B.emit("dve", tt(S_, N_, R_, ALU.mult), reads=[t1b], writes=[t1b])
    B.emit("dve", tt(S_, S_, E_, ALU.mult), reads=[t1b, SMALLb], writes=[t1b])
    B.emit("dve", ts(NSP[:, 0].rearrange("p l j -> p j l"), S_, -4.0, None, ALU.mult), reads=[t1b], writes=[NSPb])
    B.emit("dve", ts(NSP[:, 1].rearrange("p l j -> p j l"), S_, -8.0, None, ALU.mult), reads=[t1b], writes=[NSPb])
    B.emit("dve", ts(HB[:, 0].rearrange("p l j -> p j l"), PRM[:, :, prow(R_BA, 0):prow(R_BA, 0) + DEPTH], 0.5, None,
                     ALU.mult), reads=[PRMb], writes=[NSPb])
    B.emit("dve", ts(HB[:, 1].rearrange("p l j -> p j l"), PRM[:, :, prow(R_BX, 0):prow(R_BX, 0) + DEPTH], 0.5, None,
                     ALU.mult), reads=[PRMb], writes=[NSPb])

    pc, pcb = PS()
    for hh in range(2):
        t, tbuf, tds = T2K_in()
        B.dma("sp", lambda h, t=t, hh=hh: h.dma_start(out=t[0:17, :], in_=c17_d[:, hh * 512:(hh + 1) * 512]), tds,
              writes=[tbuf])
        t2, t2b = T2K()
        B.emit("act", act(t2[0:17, :], t[0:17, :], AF.Tanh, scale=0.5), reads=[tbuf], writes=[t2b])
        B.emit("dve", stt(t2[0:17, :], t2[0:17, :], 1.0, t[0:17, :], ALU.add, ALU.mult), reads=[t2b, tbuf],
               writes=[t2b])
        fns = [tr(pc[:, (hh * 4 + q) * 17:(hh * 4 + q + 1) * 17], t2[0:17, q * 128:(q + 1) * 128], IDENT[0:17, 0:17])
               for q in range(4)]
        B.emit("pe", fns, reads=[t2b, conb2], writes=[pcb])
    B.emit("act", act(CT[:].rearrange("p j r -> p (j r)"), pc[:, 0:8 * 17], AF.Copy, scale=0.5), reads=[pcb],
           writes=[CTb])

    def emit_mod(l):
        B.mark(f'mod{l}')
        m = l
        pm, pmb = ps[7], psb[7]
        for pi in range(3 * D // PW):
            sl, slb = load_piece("w_ada", l, pi * PW)
            for s in range(PW // 128):
                fc = pi * (PW // 128) + s
                fns = [mm(pm[:, fc * 17:(fc + 1) * 17], sl[:, kc, s * 128:(s + 1) * 128], CT[:, kc, :],
                          (kc == 0), (kc == 7)) for kc in range(8)]
                B.emit("pe", fns, reads=[slb, CTb], writes=[pmb])
            release_piece(slb)
            yield
        for sec in range(3):
            bias = PRM[:, :, prow(R_ADA + sec, l):prow(R_ADA + sec, l) + 1].to_broadcast([128, 8, 17])
            B.emit("dve", tt(MOD[m][:, sec], pm[:, sec * 136:(sec + 1) * 136].rearrange("p (j r) -> p j r", j=8),
                             bias, ALU.add), reads=[pmb, PRMb], writes=[MODb[m]])
        ng = PRM[:, :, prow(R_NG, l)]
        B.emit("dve", stt(DER[m][:, 0], MOD[m][:, 1, :, 0], 1.0, ng, ALU.add, ALU.mult), reads=[MODb[m], PRMb],
               writes=[DERb[m]])
        B.emit("dve", ts(DER[m][:, 1], MOD[m][:, 2, :, 0], 0.5, None, ALU.mult), reads=[MODb[m]], writes=[DERb[m]])

    def emit_ders(l):
        m = l
        ngb = PRM[:, :, prow(R_NG, l):prow(R_NG, l) + 1].to_broadcast([128, 8, NS])
        B.emit("dve", stt(DERS1[:, 0], MOD[m][:, 1, :, 1:17], 1.0, ngb, ALU.add, ALU.mult), reads=[MODb[m], PRMb],
               writes=[DERSb])
        B.emit("dve", ts(DERS1[:, 1], MOD[m][:, 2, :, 1:17], 0.5, None, ALU.mult), reads=[MODb[m]],
               writes=[DERSb])

    def load_x(hf):
        B.mark(f'loadx{hf}')
        nch = 8 + (1 if hf == 1 else 0)
        for c in range(nch):
            n = 128 if c < 8 else NS
            col0 = c * 128
            for hh in range(2):
                t, tbuf, tds = T2K_in()
                if c < 8:
                    r0 = (hf * 8 + c) * 128
                    src = xp_d[r0:r0 + 128, hh * 512:(hh + 1) * 512]
                else:
                    src = xs_d[:, hh * 512:(hh + 1) * 512]
                B.dma("sp", lambda h, t=t, n=n, src=src: h.dma_start(out=t[0:n, :], in_=src), tds, writes=[tbuf])
                p_, pb_ = PS()
                fns = [tr(p_[:, q * n:(q + 1) * n], t[0:n, q * 128:(q + 1) * 128], IDENT[0:n, 0:n]) for q in range(4)]
                B.emit("pe", fns, reads=[tbuf, conb2], writes=[pb_])
                tile = min(col0 // 512, 2)
                B.emit("act", act(X[:, hh * 4:(hh + 1) * 4, col0:col0 + n],
                                  p_[:, 0:4 * n].rearrange("p (q t) -> p q t", q=4), AF.Copy),
                       reads=[pb_], writes=[Xb[hh * 4 + q][tile] for q in range(4)])

    SETUP_H = {}

    def layer_setup_dma(l, hf):
        B.dma("sp", lambda h: h.dma_start(out=GB[:], in_=v_norm_g_d[l:l + 1, :].to_broadcast([128, D])), mds("gb"),
              writes=[GBb])
        wst = []
        for hh in range(2):
            t, tbuf, tds = T2K_in()
            B.dma("sp", lambda h, t=t, hh=hh: h.dma_start(
                out=t[:, :].rearrange("p (g s) -> p g s", g=4),
                in_=w_s_d[l, hh * 4:(hh + 1) * 4].rearrange("g t s -> t g s")), tds, writes=[tbuf])
            wst.append((t, tbuf))
        B.dma("sp", lambda h: h.dma_start(out=W00[:, :], in_=w_s_d[l, :, 0, 0:1].rearrange("g o -> o g").to_broadcast([16, 8]),
                                          allow_slow_non_contiguous=True),
              mds("w00"), writes=[WSTSb])
        bst = []
        for hh in range(2):
            t, tbuf, tds = T2K_in()
            B.dma("sp", lambda h, t=t, hh=hh: h.dma_start(
                out=t[0:1, :], in_=b_s_d[l:l + 1, hh * 4:(hh + 1) * 4].rearrange("o g t -> o (g t)")), tds,
                writes=[tbuf])
            bst.append((t, tbuf))
        for gi, wd in enumerate((w_rg_a_d, w_rg_x_d)):
            for two in range(2):
                src = wd[l].rearrange("(j two) i o -> two i j o", two=2)[two]
                dst = RG[two * 64:(two + 1) * 64, :, gi, two * 64:(two + 1) * 64]
                B.dma("pool", lambda h, dst=dst, src=src: h.dma_start(out=dst, in_=src), mds("rg"), writes=[RGb])
        SETUP_H[(l, hf)] = (wst, bst)

    def layer_setup_compute(l, hf):
        B.mark(f'setup{l}{hf}')
        wst, bst = SETUP_H.pop((l, hf))
        for hh in range(2):
            t, tbuf = wst[hh]
            p_, pb_ = PS()
            fns = [tr(p_[:, q * 128:(q + 1) * 128], t[:, q * 128:(q + 1) * 128], IDENT[:, :]) for q in range(4)]
            B.emit("pe", fns, reads=[tbuf, conb2], writes=[pb_])
            for q in range(4):
                B.emit("dve", tt(WST[:, hh * 4 + q, :], p_[:, q * 128:(q + 1) * 128], MASK[:, :], ALU.mult),
                       reads=[pb_, conb2], writes=[WSTb])
        for g in range(8):
            B.emit("dve", ts(WSTS[:, g, :], IDENT[0:16, 0:16], W00[:, g:g + 1], None, ALU.mult),
                   reads=[WSTSb, conb2], writes=[WSTSb])
        lo_t, lob = T2K()
        lo_all = lo_t[:, :].bitcast(BF16)
        los_t, losb = T2K()
        los = los_t[:, :].bitcast(BF16)
        for hh in range(2):
            t, tbuf = bst[hh]
            lo = lo_all[:, hh * 512:(hh + 1) * 512]
            bsh = BS2[0:1, hh * 4:(hh + 1) * 4, :].rearrange("o g t -> o (g t)")
            B.emit("dve", cp(bsh, t[0:1, :]), reads=[tbuf], writes=[BSb])
            B.emit("dve", tt(lo[0:1, :], t[0:1, :], bsh, ALU.subtract), reads=[tbuf, BSb], writes=[lob])
            B.emit("dve", cp(los[0:1, hh * 4 * NS:(hh + 1) * 4 * NS].rearrange("o (g t) -> o g t", g=4),
                             lo[0:1, :].rearrange("o (g t) -> o g t", g=4)[:, :, 0:1].to_broadcast([1, 4, NS])),
                   reads=[lob], writes=[losb])
            B.dma("sp", lambda h, hh=hh, lo=lo: h.dma_start(
                out=BS2[1:2, hh * 4:(hh + 1) * 4, :].rearrange("o g t -> o (g t)"), in_=lo[0:1, :]), mds("bslo"),
                reads=[lob], writes=[BSLb])
        B.emit("dve", cp(BS2S[0:1], BS2[0:1, :, 0:1].to_broadcast([1, 8, NS])), reads=[BSb], writes=[BSb])
        B.dma("sp", lambda h, los=los: h.dma_start(out=BS2S[1:2].rearrange("o g t -> o (g t)"), in_=los[0:1, 0:8 * NS]),
              mds("bslo"), reads=[losb], writes=[BSLb])

    def load_sample_state():
        B.mark('sstate')
        for l in range(DEPTH):
            p_, pb_ = PS()
            for hh in range(2):
                t, tbuf, tds = T2K_in()
                B.dma("sp", lambda h, t=t, hh=hh, l=l: h.dma_start(out=t[0:NS, :], in_=sh_d[l, :, hh * 512:(hh + 1) * 512]),
                      tds, writes=[tbuf])
                fns = [tr(p_[:, (hh * 4 + q) * NS:(hh * 4 + q + 1) * NS], t[0:NS, q * 128:(q + 1) * 128],
                          IDENT[0:NS, 0:NS]) for q in range(4)]
                B.emit("pe", fns, reads=[tbuf, conb2], writes=[pb_])
            B.emit("act", act(H0SA[:, l].rearrange("p j b -> p (j b)"), p_[:, 0:8 * NS], AF.Copy), reads=[pb_],
                   writes=[H0Sb])
            p_, pb_ = PS()
            for k in range(3):
                for hh in range(2):
                    t, tbuf, tds = T2K_in()
                    B.dma("sp", lambda h, t=t, hh=hh, k=k, l=l: h.dma_start(
                        out=t[0:NS, :], in_=sc_d[l, :, k, hh * 512:(hh + 1) * 512]), tds, writes=[tbuf])
                    fns = [tr(p_[:, (k * 8 + hh * 4 + q) * NS:(k * 8 + hh * 4 + q + 1) * NS],
                              t[0:NS, q * 128:(q + 1) * 128], IDENT[0:NS, 0:NS]) for q in range(4)]
                    B.emit("pe", fns, reads=[tbuf, conb2], writes=[pb_])
            for k in range(3):
                B.emit("act", act(SCVA[:, l, :, k, :],
                                  p_[:, k * 8 * NS:(k + 1) * 8 * NS].rearrange("p (j b) -> p j b", j=8), AF.Copy),
                       reads=[pb_], writes=[SCVb])
            B.dma("sp", lambda h, l=l: h.dma_start(out=cs_d[l, :, 0:2, :], in_=sc_d[l, :, 1:3, :]), ods(f"cs01_{l}"))

    def tiles_of(hf):
        return [(0, 512), (512, 512)] + ([(1024, NS)] if hf == 1 else [])

    def xtile(hf, ti):
        return ti

    def half_layer(l, hf, mg=None):
        def mod_step():
            if mg is not None:
                next(mg, None)

        m = l
        c0 = 0
        HC = HCL[:, l, :]
        CXB = CXBL[:, l, :, :]
        tiles = tiles_of(hf)
        nt = len(tiles)

        if l > 0:
            layer_setup_compute(l, hf)
        vsl = [load_piece("w_in", l, D + p * PW) for p in range(4)]
        B.mark(f'N{l}{hf}')
        npss = []
        for ti, (off, W) in enumerate(tiles):
            xt = xtile(hf, ti)
            pss, pssb = PS()
            npss.append((pss, pssb))
            for j in range(8):
                sq, sqb = TB()
                xin = X[:, j, c0 + off:c0 + off + W]
                if j % 2 == 0:
                    B.emit("act", act(sq[:, 0:W], xin, AF.Square), reads=[Xb[j][xt]], writes=[sqb])
                else:
                    B.emit("pool", tt(sq[:, 0:W], xin, xin, ALU.mult), reads=[Xb[j][xt]], writes=[sqb])
                B.emit("pe", mm(pss[:, 0:W], ONESB[:, :], sq[:, 0:W], (j == 0), (j == 7)), reads=[sqb, conb2],
                       writes=[pssb])
        if l == 0:
            layer_setup_dma(0, hf)
            layer_setup_compute(0, hf)
        for ti, (off, W) in enumerate(tiles):
            pss, pssb = npss[ti]
            rt, rtb = T2K()
            B.emit("act", act(rt[:, 0:W], pss[:, 0:W], AF.Ln, scale=1.0 / D, bias=SMALL[:, 0:1]), reads=[pssb, epsb],
                   writes=[rtb])
            B.emit("act", act(pss[:, 0:W], rt[:, 0:W], AF.Exp, scale=-0.5), reads=[rtb], writes=[pssb])
        for ti, (off, W) in enumerate(tiles):
            xt = xtile(hf, ti)
            pss, pssb = npss[ti]
            for j in range(8):
                xn, xnb = T2K()
                if W == 512:
                    B.emit("dve", stt(xn[:, 0:W], X[:, j, c0 + off:c0 + off + W], DER[m][:, 0, j:j + 1], pss[:, 0:W],
                                      ALU.mult, ALU.mult), reads=[Xb[j][xt], pssb, DERb[m]], writes=[xnb])
                    B.emit("act", act(H[:, j, off:off + W], xn[:, 0:W], AF.Identity, bias=MOD[m][:, 0, j, 0:1]),
                           reads=[xnb, MODb[m]], writes=[Hb[ti]])
                else:
                    B.emit("dve", tt(xn[:, 0:W], X[:, j, c0 + off:c0 + off + W], pss[:, 0:W], ALU.mult),
                           reads=[Xb[j][xt], pssb], writes=[xnb])
                    B.emit("dve", tt(xn[:, 0:W], xn[:, 0:W], DERS[m][:, 0, j, :], ALU.mult), reads=[xnb, DERSb],
                           writes=[xnb])
                    B.emit("dve", tt(H[:, j, off:off + W], xn[:, 0:W], MOD[m][:, 0, j, 1:17], ALU.add),
                           reads=[xnb, MODb[m]], writes=[Hb[ti]])

        B.mark(f'V{l}{hf}')
        regVM.handoff()
        nchunks = 8 + (1 if nt == 3 else 0)
        for c in range(nchunks):
            mod_step()
            n = 128 if c < 8 else NS
            ti = min(c // 4, 2)
            toff = c * 128
            pv = [PS(), PS()]
            for p in range(4):
                bank, bankb = pv[p // 2]
                B.emit("pe", [mm(bank[0:n, (p % 2) * PW:(p % 2 + 1) * PW], H[:, kc, toff:toff + n], vsl[p][0][:, kc, :],
                                 kc == 0, kc == 7) for kc in range(8)], reads=[Hb[ti], vsl[p][1]], writes=[bankb])
            st, stb = T2K()
            for b2 in range(2):
                junk, junkb = T2K()
                B.emit("act", act(junk[0:n, :], pv[b2][0][0:n, :], AF.Square, accum_out=st[0:n, b2:b2 + 1]),
                       reads=[pv[b2][1]], writes=[junkb, stb])
            B.emit("dve", tt(st[0:n, 2:3], st[0:n, 0:1], st[0:n, 1:2], ALU.add), reads=[stb], writes=[stb])
            B.emit("act", act(st[0:n, 3:4], st[0:n, 2:3], AF.Ln, scale=1.0 / D, bias=SMALL[0:n, 0:1]),
                   reads=[stb, epsb], writes=[stb])
            B.emit("act", act(st[0:n, 4:5], st[0:n, 3:4], AF.Exp, scale=-0.5), reads=[stb], writes=[stb])
            for b2 in range(2):
                B.emit("dve", stt(Vv[0:n, c, b2 * 512:(b2 + 1) * 512], pv[b2][0][0:n, :], st[0:n, 4:5],
                                  GB[0:n, b2 * 512:(b2 + 1) * 512], ALU.mult, ALU.mult),
                       reads=[pv[b2][1], stb, GBb], writes=[Vb[c]])
                if c == 8:
                    vs, vsb = T2K()
                    B.emit("dve", stt(vs[0:n, :], pv[b2][0][0:n, :], st[0:n, 4:5], GB[0:n, b2 * 512:(b2 + 1) * 512],
                                      ALU.mult, ALU.mult), reads=[pv[b2][1], stb, GBb], writes=[vsb])
                    B.dma("sp", lambda h, vs=vs, b2=b2: h.dma_start(out=cv_d[l, :, b2 * 512:(b2 + 1) * 512],
                                                                   in_=vs[0:NS, :]), ods_for(vsb), reads=[vsb])

        release_piece(*[v_[1] for v_ in vsl])
        pieces_B = {}

        def get_pieces_B(jp):
            if jp not in pieces_B:
                pieces_B[jp] = (load_piece("w_in", l, 3 * D + jp * PW), load_piece("w_in", l, 4 * D + jp * PW))
            return pieces_B[jp]

        def build_DG(j):
            for k in range(4):
                B.emit("dve", ts(DG[:, j % 2, k, :], IDENT[:, :], P(j, R_CW + k, l), None, ALU.mult),
                       reads=[conb2, PRMb], writes=[DGb[j % 2]])

        def gen_S1(j):
            rb = j % 2
            (slx, slxb), _ = get_pieces_B(j // 2)
            s = j % 2
            dgi = j % 2
            if j == 0:
                build_DG(0)
            if hf == 0:
                B.emit("dve", lambda h: h.memset(XBv[:, 0:3], 0.0), writes=[XBb])
            else:
                B.emit("dve", cp(XBv[:, 0:3], CXB[:, j, :]), reads=[CXBb], writes=[XBb])
            for ti, (off, W) in enumerate(tiles):
                px, pxb = PS()
                B.emit("pe", [mm(px[:, 0:W], slx[:, kc, s * 128:(s + 1) * 128], H[:, kc, off:off + W], kc == 0,
                                 kc == 7) for kc in range(8)], reads=[slxb, Hb[ti]], writes=[pxb])
                yield
                if ti == 1 and j + 1 < 8:
                    build_DG(j + 1)
                pcv, pcvb = PS()
                if W == 512:
                    B.emit("act", act(XBv[:, 3 + off:3 + off + W], px[:, 0:W], AF.Copy), reads=[pxb], writes=[XBb])
                    if hf == 1 and ti == 1:
                        B.emit("dve", cp(CPT[:, j, :], px[:, W - 3:W]), reads=[pxb, XBb], writes=[CPTb])
                    if hf == 0 and ti == 1:
                        B.emit("dve", cp(CXB[:, j, :], px[:, W - 3:W]), reads=[pxb, XBb], writes=[CXBb])
                    B.emit("pe", [mm(pcv[:, 0:W], DG[:, dgi, k, :], XBv[:, off + k:off + k + W], k == 0, k == 3)
                                  for k in range(4)], reads=[DGb[dgi], XBb], writes=[pcvb])
                else:
                    B.emit("act", act(SCN[:, j, :], px[:, 0:W], AF.Copy), reads=[pxb], writes=[SCNb])
                    B.emit("dve", cp(CSS[:, j, :], px[:, 0:W]), reads=[pxb, SCNb], writes=[CSSb])
                    B.emit("pe", [mm(pcv[:, 0:W], DG[:, dgi, k, :], (SCVA[:, l, j, k, :] if k < 3 else SCN[:, j, :]),
                                     k == 0, k == 3) for k in range(4)], reads=[DGb[dgi], SCVb, SCNb], writes=[pcvb])
                yield
                xc, xcb_ = T2K()
                B.emit("act", act(xc[:, 0:W], pcv[:, 0:W], AF.Identity, bias=P(j, R_CB, l)), reads=[pcvb, PRMb],
                       writes=[xcb_])
                xh, xhb = TB()
                B.emit("act", act(xh[:, 0:W], pcv[:, 0:W], AF.Identity, bias=P(j, R_CB, l)), reads=[pcvb, PRMb],
                       writes=[xhb])
                pr, prb = PS()
                B.emit("pe", mm(pr[:, 0:W], RG[:, j, 0, :], xh[:, 0:W], True, True), reads=[RGb, xhb], writes=[prb])
                pi_, pib = PS()
                B.emit("pe", mm(pi_[:, 0:W], RG[:, j, 1, :], xh[:, 0:W], True, True), reads=[RGb, xhb], writes=[pib])
                yield
                trr, trb = T2K()
                B.emit("act", act(trr[:, 0:W], pr[:, 0:W], AF.Tanh, scale=0.5, bias=HB[:, 0, l, j:j + 1]),
                       reads=[prb, NSPb], writes=[trb])
                B.emit("act", act(ROW_S[:, off:off + W], trr[:, 0:W], AF.Exp, scale=NSP[:, 1, l, j:j + 1],
                                  bias=NSP[:, 1, l, j:j + 1]), reads=[trb, NSPb], writes=[rowSb[ti]])
                B.emit("act", act(ROW_A[rb][:, off:off + W], trr[:, 0:W], AF.Exp, scale=NSP[:, 0, l, j:j + 1],
                                  bias=NSP[:, 0, l, j:j + 1]), reads=[trb, NSPb], writes=[rowAb[rb][ti]])
                tii, tib = T2K()
                B.emit("act", act(tii[:, 0:W], pi_[:, 0:W], AF.Tanh, scale=0.5, bias=HB[:, 1, l, j:j + 1]),
                       reads=[pib, NSPb], writes=[tib])
                B.emit("dve", stt(ROW_P[rb][:, off:off + W], tii[:, 0:W], 1.0, xc[:, 0:W], ALU.add, ALU.mult),
                       reads=[tib, xcb_], writes=[rowPb[rb][ti]])
            if j % 2 == 1:
                release_piece(slxb)

        def emit_S2(j):
            rb = j % 2
            for ti, (off, W) in enumerate(tiles):
                B.emit("act", act(ROW_S[:, off:off + W], ROW_S[:, off:off + W], AF.Sqrt, scale=-0.25,
                                  bias=SMALL[:, 1:2]), reads=[rowSb[ti], epsb], writes=[rowSb[ti]])
            for ti, (off, W) in enumerate(tiles):
                B.emit("dve", tt(ROW_P[rb][:, off:off + W], ROW_P[rb][:, off:off + W], ROW_S[:, off:off + W],
                                 ALU.mult), reads=[rowPb[rb][ti], rowSb[ti]], writes=[rowPb[rb][ti]])

        def gen_S3(j):
            rb = j % 2
            _, (slg, slgb) = get_pieces_B(j // 2)
            s = j % 2
            for ti, (off, W) in enumerate(tiles):
                hs_, hsb = T2K()
                if W == 512:
                    if hf == 0 and ti == 0:
                        init = 0.0
                        rd = []
                    else:
                        init = HC[:, j:j + 1]
                        rd = [HCb]
                    B.emit("dve", lambda h, hs_=hs_, off=off, W=W, init=init, rb=rb: h.tensor_tensor_scan(
                        out=hs_[:, 0:W], data0=ROW_A[rb][:, off:off + W], data1=ROW_P[rb][:, off:off + W],
                        initial=init, op0=ALU.mult, op1=ALU.add), reads=[rowAb[rb][ti], rowPb[rb][ti]] + rd,
                        writes=[hsb])
                    if hf == 1 and ti == 1:
                        B.emit("dve", cp(HP[:, j:j + 1], hs_[:, W - 1:W]), reads=[hsb], writes=[HPb])
                    else:
                        B.emit("dve", cp(HC[:, j:j + 1], hs_[:, W - 1:W]), reads=[hsb], writes=[HCb])
                else:
                    B.emit("dve", tt(hs_[:, 0:W], ROW_A[rb][:, off:off + W], H0SA[:, l, j, :], ALU.mult),
                           reads=[rowAb[rb][ti], H0Sb], writes=[hsb])
                    B.emit("dve", tt(hs_[:, 0:W], hs_[:, 0:W], ROW_P[rb][:, off:off + W], ALU.add),
                           reads=[hsb, rowPb[rb][ti]], writes=[hsb])
                    B.emit("dve", cp(HSS[:, j, :], hs_[:, 0:W]), reads=[hsb], writes=[HSSb])
                pg, pgb = PS()
                B.emit("pe", [mm(pg[:, 0:W], slg[:, kc, s * 128:(s + 1) * 128], H[:, kc, off:off + W], kc == 0,
                                 kc == 7) for kc in range(8)], reads=[slgb, Hb[ti]], writes=[pgb])
                yield
                tg, tgb = T2K()
                B.emit("act", act(tg[:, 0:W], pg[:, 0:W], AF.Tanh, scale=0.5), reads=[pgb], writes=[tgb])
                B.emit("dve", stt(tg[:, 0:W], tg[:, 0:W], 1.0, pg[:, 0:W], ALU.add, ALU.mult), reads=[tgb, pgb],
                       writes=[tgb])
                B.emit("dve", stt(YB[:, j, off:off + W], hs_[:, 0:W], 0.5, tg[:, 0:W], ALU.mult, ALU.mult),
                       reads=[hsb, tgb], writes=[YBb[j][ti]])
            if j % 2 == 1:
                release_piece(slgb)

        def gen_B():
            yield from gen_S1(0)
            emit_S2(0)
            yield 3
            for j in range(8):
                g3 = gen_S3(j)
                g1 = gen_S1(j + 1) if j + 1 < 8 else iter(())
                a3 = a1 = True
                while a3 or a1:
                    if a1:
                        try:
                            next(g1)
                            yield
                        except StopIteration:
                            a1 = False
                    if a3:
                        try:
                            next(g3)
                            yield
                        except StopIteration:
                            a3 = False
                if j + 1 < 8:
                    emit_S2(j + 1)
                    yield 3

        def gen_A():
            for gp in range(4):
                slu, slub = load_piece("w_in", l, 0 * D + gp * PW)
                sla, slab = load_piece("w_in", l, 2 * D + gp * PW)
                for s in range(2):
                    g = gp * 2 + s
                    for ti, (off, W) in enumerate(tiles):
                        pu, pub = PS()
                        B.emit("pe", [mm(pu[:, 0:W], slu[:, kc, s * 128:(s + 1) * 128], H[:, kc, off:off + W], kc == 0,
                                         kc == 7) for kc in range(8)], reads=[slub, Hb[ti]], writes=[pub])
                        yield
                        pga, pgab = PS()
                        B.emit("pe", [mm(pga[:, 0:W], sla[:, kc, s * 128:(s + 1) * 128], H[:, kc, off:off + W], kc == 0,
                                         kc == 7) for kc in range(8)], reads=[slab, Hb[ti]], writes=[pgab])
                        yield
                        pS, pSb = PS()
                        if W == 512:
                            fns = [mm(pS[:, 0:512], ONESB[0:2, :], BS2[0:2, g:g + 1, :].to_broadcast([2, 4, 128]),
                                      True, False)]
                            rds = [WSTb, BSb, BSLb, conb2]
                            for q in range(4):
                                c = (off // 128) + q
                                fns.append(mm(pS[:, q * 128:(q + 1) * 128], Vv[:, c, g * 128:(g + 1) * 128],
                                              WST[:, g, :], False, q == 3))
                                rds.append(Vb[c])
                            B.emit("pe", fns, reads=rds, writes=[pSb])
                        else:
                            o = pS[:, 0:NS]
                            fns = [mm(o, ONESB[0:2, :], BS2S[0:2, g, :], True, False),
                                   mm(o, Vv[0:NS, 8, g * 128:(g + 1) * 128], WSTS[:, g, :], False, True)]
                            B.emit("pe", fns, reads=[WSTSb, BSb, BSLb, conb2, Vb[8]], writes=[pSb])
                        tg, tgb = T2K()
                        B.emit("act", act(tg[:, 0:W], pga[:, 0:W], AF.Tanh, scale=0.5), reads=[pgab], writes=[tgb])
                        B.emit("dve", stt(tg[:, 0:W], tg[:, 0:W], 1.0, pga[:, 0:W], ALU.add, ALU.mult), reads=[tgb, pgab],
                               writes=[tgb])
                        B.emit("dve", tt(tg[:, 0:W], pu[:, 0:W], tg[:, 0:W], ALU.mult), reads=[pub, tgb], writes=[tgb])
                        B.emit("dve", stt(YA[:, g, off:off + W], tg[:, 0:W], 0.5, pS[:, 0:W], ALU.mult, ALU.mult),
                               reads=[tgb, pSb], writes=[YAb[g][ti]])
                        yield
                    if s == 1 and ti == nt - 1:
                        release_piece(slub, slab)


        B.mark(f'BA{l}{hf}')
        gens = [gen_B(), gen_A()]
        alive = [True, True]
        while alive[0] or alive[1]:
            extra = 0
            if alive[0]:
                try:
                    extra = next(gens[0]) or 0
                except StopIteration:
                    alive[0] = False
            for _ in range(1 + extra):
                if alive[1]:
                    try:
                        next(gens[1])
                    except StopIteration:
                        alive[1] = False

        B.mark(f'M{l}{hf}')
        regVM.handoff()
        for np_ in range(4):
            mod_step()
            sza, szab = load_piece("w_in", l, 5 * D + np_ * PW)
            spa, spab = load_piece("w_pa", l, np_ * PW)
            szb, szbb = load_piece("w_in", l, 6 * D + np_ * PW)
            spb, spbb = load_piece("w_pb", l, np_ * PW)
            for s in range(2):
                n_ = np_ * 2 + s
                cs = slice(s * 128, (s + 1) * 128)
                for ti, (off, W) in enumerate(tiles):
                    res = []
                    for (sz, szb_, sp_, spb_, Y, Yb) in ((sza, szab, spa, spab, YA, YAb), (szb, szbb, spb, spbb, YB, YBb)):
                        pz, pzb = PS()
                        B.emit("pe", [mm(pz[:, 0:W], sz[:, kc, cs], H[:, kc, off:off + W], kc == 0, kc == 7)
                                      for kc in range(8)], reads=[szb_, Hb[ti]], writes=[pzb])
                        pp_, ppb_ = PS()
                        B.emit("pe", [mm(pp_[:, 0:W], sp_[:, kc, cs], Y[:, kc, off:off + W], kc == 0, kc == 7)
                                      for kc in range(8)], reads=[spb_] + [Yb[kc][ti] for kc in range(8)], writes=[ppb_])
                        tz, tzb = T2K()
                        B.emit("act", act(tz[:, 0:W], pz[:, 0:W], AF.Tanh, scale=0.5), reads=[pzb], writes=[tzb])
                        B.emit("dve", stt(tz[:, 0:W], tz[:, 0:W], 1.0, pp_[:, 0:W], ALU.add, ALU.mult),
                               reads=[tzb, ppb_], writes=[tzb])
                        res.append((tz, tzb))
                    B.emit("dve", tt(Mv[:, n_, off:off + W], res[0][0][:, 0:W], res[1][0][:, 0:W], ALU.add),
                           reads=[res[0][1], res[1][1]], writes=[Mb[ti]])
            release_piece(szab, spab, szbb, spbb)

        if l + 1 < DEPTH:
            layer_setup_dma(l + 1, hf)
        B.mark(f'O{l}{hf}')
        for np_ in range(4):
            so, sob = load_piece("w_out", l, np_ * PW)
            for s in range(2):
                n_ = np_ * 2 + s
                for ti, (off, W) in enumerate(tiles):
                    xt = xtile(hf, ti)
                    po, pob = PS()
                    B.emit("pe", [mm(po[:, 0:W], so[:, kc, s * 128:(s + 1) * 128], Mv[:, kc, off:off + W], kc == 0,
                                     kc == 7) for kc in range(8)], reads=[sob, Mb[ti]], writes=[pob])
                    xa = X[:, n_, c0 + off:c0 + off + W]
                    if W == 512:
                        B.emit("dve", stt(xa, po[:, 0:W], DER[m][:, 1, n_:n_ + 1], xa, ALU.mult, ALU.add),
                               reads=[pob, DERb[m], Xb[n_][xt]], writes=[Xb[n_][xt]])
                    else:
                        tmp, tmpb = OTMP, OTMPb
                        B.emit("dve", tt(tmp[:, 0:W], po[:, 0:W], DERS[m][:, 1, n_, :], ALU.mult), reads=[pob, DERSb],
                               writes=[tmpb])
                        B.emit("dve", tt(xa, xa, tmp[:, 0:W], ALU.add), reads=[tmpb, Xb[n_][xt]], writes=[Xb[n_][xt]])
            release_piece(sob)

    def fm16_to_rows(src, srcb, dram_rows, name):
        for hh in range(2):
            p_, pb_ = PS()
            fns = [tr(p_[0:NS, q * 128:(q + 1) * 128], src[:, hh * 4 + q, :], IDENT[:, :]) for q in range(4)]
            B.emit("pe", fns, reads=[srcb, conb2], writes=[pb_])
            t, tbuf = T2K()
            B.emit("act", act(t[0:NS, :], p_[0:NS, :], AF.Copy), reads=[pb_], writes=[tbuf])
            B.dma("sp", lambda h, t=t, hh=hh: h.dma_start(out=dram_rows[:, hh * 512:(hh + 1) * 512], in_=t[0:NS, :]),
                  ods_for(tbuf), reads=[tbuf])

    def layer_outputs(l):
        B.mark(f'out{l}')
        B.dma("sp", lambda h: h.dma_start(out=hp_d[l].rearrange("(j p) -> p j", p=128), in_=HP[:, :],
                                          allow_slow_non_contiguous=True), ods_for(HPb), reads=[HPb])
        for k in range(3):
            B.dma("sp", lambda h, k=k: h.dma_start(out=cp_d[l, k].rearrange("(j p) -> p j", p=128), in_=CPT[:, :, k],
                                                   allow_slow_non_contiguous=True), ods_for(CPTb), reads=[CPTb])
        fm16_to_rows(HSS, HSSb, hs_d[l], f"hs{l}")
        fm16_to_rows(CSS, CSSb, cs_d[l, :, 2, :], f"cs2_{l}")

    def final_out(hf):
        B.mark(f'final{hf}')
        B.dma("sp", lambda h: h.dma_start(out=GB[:], in_=final_g_d[0:1, :].to_broadcast([128, D])), mds("gb"),
              writes=[GBb])
        nch = 8 + (1 if hf == 1 else 0)
        for c in range(nch):
            n = 128 if c < 8 else NS
            col0 = c * 128
            xt = min(col0 // 512, 2)
            pv = [PS(), PS()]
            for hh in range(2):
                fns = [tr(pv[hh][0][0:n, q * 128:(q + 1) * 128], X[:, hh * 4 + q, col0:col0 + n], IDENT[:, :])
                       for q in range(4)]
                B.emit("pe", fns, reads=[Xb[hh * 4 + q][xt] for q in range(4)] + [conb2], writes=[pv[hh][1]])
            st, stb = T2K()
            for b2 in range(2):
                junk, junkb = T2K()
                B.emit("act", act(junk[0:n, :], pv[b2][0][0:n, :], AF.Square, accum_out=st[0:n, b2:b2 + 1]),
                       reads=[pv[b2][1]], writes=[junkb, stb])
            B.emit("dve", tt(st[0:n, 2:3], st[0:n, 0:1], st[0:n, 1:2], ALU.add), reads=[stb], writes=[stb])
            B.emit("act", act(st[0:n, 3:4], st[0:n, 2:3], AF.Sqrt, scale=1.0 / D, bias=SMALL[0:n, 0:1]),
                   reads=[stb, epsb], writes=[stb])
            B.emit("dve", lambda h, st=st, n=n: h.reciprocal(out=st[0:n, 4:5], in_=st[0:n, 3:4]), reads=[stb],
                   writes=[stb])
            for b2 in range(2):
                o, ob = T2K()
                B.emit("dve", stt(o[0:n, :], pv[b2][0][0:n, :], st[0:n, 4:5], GB[0:n, b2 * 512:(b2 + 1) * 512],
                                  ALU.mult, ALU.mult), reads=[pv[b2][1], stb, GBb], writes=[ob])
                if c < 8:
                    r0 = (hf * 8 + c) * 128
                    dst = yp_d[r0:r0 + 128, b2 * 512:(b2 + 1) * 512]
                else:
                    dst = ys_d[:, b2 * 512:(b2 + 1) * 512]
                B.dma("sp", lambda h, o=o, n=n, dst=dst: h.dma_start(out=dst, in_=o[0:n, :]), ods_for(ob),
                      reads=[ob])

    epsb = Buf("eps", track=False)
    B.emit("dve", lambda h: h.memset(SMALL[:, 0:1], EPS), reads=[SMALLb, t1b], writes=[SMALLb])
    B.emit("dve", lambda h: h.memset(SMALL[:, 1:2], 0.25), writes=[SMALLb, epsb])

    load_x(0)
    for _ in emit_mod(0):
        pass
    for hf in range(2):
        if hf == 1:
            load_sample_state()
            load_x(1)
        for l in range(DEPTH):
            if hf == 1:
                emit_ders(l)
            mg = emit_mod(l + 1) if (hf == 0 and l + 1 < DEPTH) else None
            half_layer(l, hf, mg)
            if mg is not None:
                for _ in mg:
                    pass
            if hf == 1:
                layer_outputs(l)
        final_out(hf)

    fin = []
    for d in out_ds:
        if d.n > 0:
            fin.append((d.sem, 16 * d.n))
    B.ops["sp"].append((fin, [], None))

    block = B.es.enter_context(nc.Block())

    @block.tensor
    def _(h):
        B.replay("pe", h)

    @block.scalar
    def _(h):
        B.replay("act", h)

    @block.vector
    def _(h):
        B.replay("dve", h)

    @block.gpsimd
    def _(h):
        B.replay("pool", h)

    @block.sync
    def _(h):
        B.replay("sp", h)

    B.es.close()
    if _osx.environ.get("KMARK"):
        import json as _json
        _json.dump(B.marks, open(_osx.environ["KMARK"], "w"))
    return nc


_NC_CACHE = {}


def kernel(x_prompt, x_sample, c_prompt, c_sample, state_rglru_h, state_conv,
           w_ada, b_ada, norm_g, w_in, v_norm_g, w_s, b_s, conv_w, conv_b,
           w_rg_a, b_rg_a, w_rg_x, b_rg_x, lam, w_pa, w_pb, w_out, final_g):
    f = lambda a: np.ascontiguousarray(np.asarray(a), dtype=np.float32)
    if "nc" not in _NC_CACHE:
        _NC_CACHE["nc"] = build_program()
    nc = _NC_CACHE["nc"]
    ident = np.eye(128, dtype=np.float32)
    mask = np.triu(np.ones((128, 128), dtype=np.float32))
    shared = {
        "w_ada": f(w_ada), "b_ada": f(b_ada), "norm_g": f(norm_g), "w_in": f(w_in), "v_norm_g": f(v_norm_g),
        "w_s": f(w_s), "b_s": f(b_s), "conv_w": f(conv_w), "conv_b": f(conv_b), "w_rg_a": f(w_rg_a),
        "b_rg_a": f(b_rg_a), "w_rg_x": f(w_rg_x), "b_rg_x": f(b_rg_x), "lam": f(lam), "w_pa": f(w_pa),
        "w_pb": f(w_pb), "w_out": f(w_out), "final_g": f(final_g).reshape(1, D), "ident": ident, "mask": mask,
    }
    x_prompt = f(x_prompt)
    x_sample = f(x_sample)
    c_prompt = f(c_prompt)
    c_sample = f(c_sample)
    state_rglru_h = f(state_rglru_h)
    state_conv = f(state_conv)
    in_maps = []
    for c in range(NCORES):
        r = slice(c * NS, (c + 1) * NS)
        m = dict(shared)
        m["xp"] = np.ascontiguousarray(x_prompt[c])
        m["xs"] = np.ascontiguousarray(x_sample[r, 0, :])
        m["c17"] = np.ascontiguousarray(np.concatenate([c_prompt[c:c + 1], c_sample[r]], axis=0))
        m["sh"] = np.ascontiguousarray(state_rglru_h[:, r, :])
        m["sc"] = np.ascontiguousarray(state_conv[:, r, :, :])
        in_maps.append(m)
    res = run_bass_kernel_spmd(nc, in_maps, core_ids=list(range(NCORES)))
    R = res.results
    y_prompt = np.stack([R[c]["yp"] for c in range(NCORES)], axis=0)
    y_sample = np.concatenate([R[c]["ys"] for c in range(NCORES)], axis=0)[:, None, :]
    h_prompt = np.stack([R[c]["hp"] for c in range(NCORES)], axis=1)
    conv_prompt = np.stack([R[c]["cp"] for c in range(NCORES)], axis=1)
    h_sample = np.concatenate([R[c]["hs"] for c in range(NCORES)], axis=1)
    conv_sample = np.concatenate([R[c]["cs"] for c in range(NCORES)], axis=1)
    chunk_v = np.concatenate([R[c]["cv"] for c in range(NCORES)], axis=1)[:, :, None, :]
    return (y_prompt.astype(np.float32), y_sample.astype(np.float32), h_prompt.astype(np.float32),
            conv_prompt.astype(np.float32), h_sample.astype(np.float32), conv_sample.astype(np.float32),
            chunk_v.astype(np.float32))
```
